# Optimizing a Trainium2 kernel written in Bass

```python
import jax, jax.numpy as jnp
from jax import lax
import numpy as np

D_MODEL = 1024
BATCH = 4
SEQ = 4096
DEPTH = 1

HEAD_DIM = 64
MIX_WIDTH = D_MODEL
NSA_WIDTH = MIX_WIDTH // 2
DSA_WIDTH = MIX_WIDTH - NSA_WIDTH
NSA_HEADS = NSA_WIDTH // HEAD_DIM
NSA_KV_HEADS = 2
NSA_KV_W = NSA_KV_HEADS * HEAD_DIM
DSA_HEADS = DSA_WIDTH // HEAD_DIM
DSA_KV_HEADS = 2
DSA_KV_W = DSA_KV_HEADS * HEAD_DIM
IDX_HEADS = 4
IDX_DIM = 64
CMP_BLOCK = 32
CMP_STRIDE = 16
CMP_HIDDEN = 256
SEL_BLOCK = 64
SEL_TOPN = 16
WINDOW = 512
DSA_TOPK_MAX = 256
ROPE_THETA = 10000.0
EPS = 1e-6
Q_BLOCK = 128
SEL_Q_BLOCK = 64
NEG = -1e30
FORCE = 1e6
ATTN_SCALE = HEAD_DIM ** -0.5

IN_WIDTHS = (NSA_WIDTH, NSA_KV_W, NSA_KV_W, NSA_KV_W, NSA_KV_W, NSA_KV_W, NSA_KV_W,
             3 * NSA_HEADS, NSA_WIDTH,
             DSA_WIDTH, DSA_KV_W, DSA_KV_W, IDX_HEADS * IDX_DIM, IDX_DIM, IDX_HEADS, DSA_WIDTH)
IN_COLS = sum(IN_WIDTHS)

kernel_name = "hymba_nsa_dsa_hybrid_layer"


def rms_norm(x, gain):
    xf = x.astype(jnp.float32)
    y = xf * lax.rsqrt(jnp.mean(xf * xf, axis=-1, keepdims=True) + EPS)
    return (y * gain.astype(jnp.float32)).astype(x.dtype)


def rope(x, pos):
    half = x.shape[-1] // 2
    inv_freq = ROPE_THETA ** (-jnp.arange(half, dtype=jnp.float32) / half)
    ang = pos[:, None] * inv_freq[None, :]
    cos = jnp.cos(ang)[:, None, :]
    sin = jnp.sin(ang)[:, None, :]
    xf = x.astype(jnp.float32)
    x1, x2 = xf[..., :half], xf[..., half:]
    return jnp.concatenate([x1 * cos - x2 * sin, x2 * cos + x1 * sin], axis=-1).astype(x.dtype)


def masked_softmax(s, mask):
    s = jnp.where(mask, s.astype(jnp.float32), NEG)
    p = jax.nn.softmax(s, axis=-1)
    return jnp.where(mask, p, 0.0)


def split_columns(proj, widths):
    offsets = np.cumsum(np.array(widths))[:-1]
    return jnp.split(proj, [int(o) for o in offsets], axis=-1)


def compress(k_raw, pe, w1, b1, w2):
    B, S, G, Dh = k_raw.shape
    n_cmp = (S - CMP_BLOCK) // CMP_STRIDE + 1
    idx = jnp.arange(n_cmp)[:, None] * CMP_STRIDE + jnp.arange(CMP_BLOCK)[None, :]
    blocks = k_raw[:, idx] + pe[None, None, :, None, :]
    flat = blocks.transpose(0, 1, 3, 2, 4).reshape(B, n_cmp, G, CMP_BLOCK * Dh)
    hid = jax.nn.silu(flat @ w1 + b1)
    return hid @ w2


def compressed_branch(q, kc, vc):
    S = q.shape[1]
    n_cmp = kc.shape[1]
    cmp_end = jnp.arange(n_cmp) * CMP_STRIDE + CMP_BLOCK - 1
    t = jnp.arange(S)
    s = jnp.einsum('bsgrd,bcgd->bgrsc', q, kc) * ATTN_SCALE
    p = masked_softmax(s, cmp_end[None, :] <= t[:, None])
    o = jnp.einsum('bgrsc,bcgd->bsgrd', p.astype(vc.dtype), vc)
    return o, p


def selection_indices(p_cmp):
    S, n_cmp = p_cmp.shape[-2], p_cmp.shape[-1]
    n_sel = S // SEL_BLOCK
    c_start = jnp.arange(n_cmp) * CMP_STRIDE
    j = jnp.arange(n_sel)
    j_start = j * SEL_BLOCK
    overlap = jnp.clip(jnp.minimum(c_start[:, None] + CMP_BLOCK, j_start[None, :] + SEL_BLOCK)
                       - jnp.maximum(c_start[:, None], j_start[None, :]), 0, None)
    overlap = overlap.astype(jnp.float32) / CMP_BLOCK
    imp = jnp.einsum('bgrsc,cj->bgsj', p_cmp, overlap)
    t = jnp.arange(S)
    cur = t // SEL_BLOCK
    forced = (j[None, :] == 0) | (j[None, :] == cur[:, None]) | (j[None, :] == cur[:, None] - 1)
    imp = jnp.where(forced, FORCE, imp)
    imp = jnp.where(j_start[None, :] <= t[:, None], imp, NEG)
    _, idx = lax.top_k(imp, min(SEL_TOPN, n_sel))
    return idx.transpose(0, 2, 1, 3)


def selected_branch(q, ks, vs, sel_idx):
    B, S, G, R, Dh = q.shape
    n_sel = S // SEL_BLOCK
    n = sel_idx.shape[-1]
    k_blocks = ks.reshape(B, n_sel, SEL_BLOCK, G, Dh).transpose(0, 3, 1, 2, 4)
    v_blocks = vs.reshape(B, n_sel, SEL_BLOCK, G, Dh).transpose(0, 3, 1, 2, 4)
    b_ix = jnp.arange(B)[:, None, None, None]
    g_ix = jnp.arange(G)[None, None, :, None]

    def one_block(i):
        t0 = i * SEL_Q_BLOCK
        q_b = lax.dynamic_slice_in_dim(q, t0, SEL_Q_BLOCK, axis=1)
        idx_b = lax.dynamic_slice_in_dim(sel_idx, t0, SEL_Q_BLOCK, axis=1)
        k_sel = k_blocks[b_ix, g_ix, idx_b].reshape(B, SEL_Q_BLOCK, G, n * SEL_BLOCK, Dh)
        v_sel = v_blocks[b_ix, g_ix, idx_b].reshape(B, SEL_Q_BLOCK, G, n * SEL_BLOCK, Dh)
        kpos = (idx_b[..., None] * SEL_BLOCK + jnp.arange(SEL_BLOCK)).reshape(B, SEL_Q_BLOCK, G, n * SEL_BLOCK)
        t = t0 + jnp.arange(SEL_Q_BLOCK)
        mask = (kpos <= t[None, :, None, None])[:, :, :, None, :]
        s = jnp.einsum('btgrd,btgkd->btgrk', q_b, k_sel) * ATTN_SCALE
        p = masked_softmax(s, mask)
        return jnp.einsum('btgrk,btgkd->btgrd', p.astype(v_sel.dtype), v_sel)

    out = lax.map(one_block, jnp.arange(S // SEL_Q_BLOCK))
    return out.transpose(1, 0, 2, 3, 4, 5).reshape(B, S, G, R, Dh)


def window_branch(q, kw, vw):
    B, S, G, R, Dh = q.shape
    pad = ((0, 0), (WINDOW, 0), (0, 0), (0, 0))
    kp = jnp.pad(kw, pad)
    vp = jnp.pad(vw, pad)
    span = Q_BLOCK + WINDOW

    def one_block(i):
        t0 = i * Q_BLOCK
        q_b = lax.dynamic_slice_in_dim(q, t0, Q_BLOCK, axis=1)
        k_b = lax.dynamic_slice_in_dim(kp, t0, span, axis=1)
        v_b = lax.dynamic_slice_in_dim(vp, t0, span, axis=1)
        kpos = t0 - WINDOW + jnp.arange(span)
        t = t0 + jnp.arange(Q_BLOCK)
        diff = t[:, None] - kpos[None, :]
        mask = (diff >= 0) & (diff < WINDOW) & (kpos[None, :] >= 0)
        s = jnp.einsum('btgrd,bkgd->bgrtk', q_b, k_b) * ATTN_SCALE
        p = masked_softmax(s, mask)
        return jnp.einsum('bgrtk,bkgd->btgrd', p.astype(v_b.dtype), v_b)

    out = lax.map(one_block, jnp.arange(S // Q_BLOCK))
    return out.transpose(1, 0, 2, 3, 4, 5).reshape(B, S, G, R, Dh)


def dsa_branch(q, k, v, qi, ki, wi):
    B, S, G, R, Dh = q.shape
    topk = min(DSA_TOPK_MAX, S // 4)
    b_ix = jnp.arange(B)[:, None, None]
    key_pos = jnp.arange(S)
    wi = wi * (IDX_HEADS ** -0.5)

    def one_block(i):
        t0 = i * Q_BLOCK
        q_b = lax.dynamic_slice_in_dim(q, t0, Q_BLOCK, axis=1)
        qi_b = lax.dynamic_slice_in_dim(qi, t0, Q_BLOCK, axis=1)
        wi_b = lax.dynamic_slice_in_dim(wi, t0, Q_BLOCK, axis=1)
        t = t0 + jnp.arange(Q_BLOCK)
        logits = jnp.einsum('bthd,bsd->bths', qi_b, ki) * (IDX_DIM ** -0.5)
        score = jnp.einsum('bths,bth->bts', jax.nn.relu(logits), wi_b).astype(jnp.float32)
        score = jnp.where(key_pos[None, None, :] <= t[None, :, None], score, NEG)
        _, idx = lax.top_k(score, topk)
        k_sel = k[b_ix, idx]
        v_sel = v[b_ix, idx]
        mask = (idx <= t[None, :, None])[:, :, None, None, :]
        s = jnp.einsum('btgrd,btkgd->btgrk', q_b, k_sel) * ATTN_SCALE
        p = masked_softmax(s, mask)
        return jnp.einsum('btgrk,btkgd->btgrd', p.astype(v_sel.dtype), v_sel)

    out = lax.map(one_block, jnp.arange(S // Q_BLOCK))
    return out.transpose(1, 0, 2, 3, 4, 5).reshape(B, S, G, R, Dh)


def hybrid_layer(x, norm_gain, w_in, nsa_q_gain, nsa_kc_gain, nsa_ks_gain, nsa_kw_gain,
                 cmp_pe_k, cmp_k_w1, cmp_k_b1, cmp_k_w2, cmp_pe_v, cmp_v_w1, cmp_v_b1, cmp_v_w2,
                 dsa_q_gain, dsa_k_gain, w_out):
    B, S, _ = x.shape
    pos = jnp.arange(S, dtype=jnp.float32)
    h = rms_norm(x, norm_gain)
    proj = jnp.einsum('bsd,dc->bsc', h, w_in)
    (q_n, kc, vc, ks, vs, kw, vw, gate_logits, z_n,
     q_d, k_d, v_d, qi, ki, wi, z_d) = split_columns(proj, IN_WIDTHS)

    def heads(a, n):
        return a.reshape(B, S, n, HEAD_DIM)

    G, R = NSA_KV_HEADS, NSA_HEADS // NSA_KV_HEADS
    qn = rope(rms_norm(heads(q_n, NSA_HEADS), nsa_q_gain), pos).reshape(B, S, G, R, HEAD_DIM)
    kc_cmp = compress(heads(kc, G), cmp_pe_k, cmp_k_w1, cmp_k_b1, cmp_k_w2)
    vc_cmp = compress(heads(vc, G), cmp_pe_v, cmp_v_w1, cmp_v_b1, cmp_v_w2)
    cmp_pos = (jnp.arange(kc_cmp.shape[1]) * CMP_STRIDE + CMP_BLOCK - 1).astype(jnp.float32)
    kc_cmp = rope(rms_norm(kc_cmp, nsa_kc_gain), cmp_pos)
    o_cmp, p_cmp = compressed_branch(qn, kc_cmp, vc_cmp)
    sel_idx = selection_indices(p_cmp)
    ks_r = rope(rms_norm(heads(ks, G), nsa_ks_gain), pos)
    o_sel = selected_branch(qn, ks_r, heads(vs, G), sel_idx)
    kw_r = rope(rms_norm(heads(kw, G), nsa_kw_gain), pos)
    o_win = window_branch(qn, kw_r, heads(vw, G))
    g = jax.nn.sigmoid(gate_logits.astype(jnp.float32)).reshape(B, S, G, R, 3).astype(o_cmp.dtype)
    o_nsa = g[..., 0:1] * o_cmp + g[..., 1:2] * o_sel + g[..., 2:3] * o_win
    o_nsa = o_nsa.reshape(B, S, NSA_WIDTH) * jax.nn.silu(z_n)

    Gd, Rd = DSA_KV_HEADS, DSA_HEADS // DSA_KV_HEADS
    qd = rope(rms_norm(heads(q_d, DSA_HEADS), dsa_q_gain), pos).reshape(B, S, Gd, Rd, HEAD_DIM)
    kd = rope(rms_norm(heads(k_d, Gd), dsa_k_gain), pos)
    qi_r = rope(qi.reshape(B, S, IDX_HEADS, IDX_DIM), pos)
    ki_r = rope(ki.reshape(B, S, 1, IDX_DIM), pos).reshape(B, S, IDX_DIM)
    o_dsa = dsa_branch(qd, kd, heads(v_d, Gd), qi_r, ki_r, wi)
    o_dsa = o_dsa.reshape(B, S, DSA_WIDTH) * jax.nn.silu(z_d)

    mixed = jnp.concatenate([o_nsa, o_dsa], axis=-1)
    return x + jnp.einsum('bsc,cd->bsd', mixed, w_out)


def setup_inputs(seed: int = 0) -> dict:
    key = jax.random.key(seed)
    k = jax.random.split(key, 18)
    L = DEPTH
    Dh = HEAD_DIM

    def nrm(kk, shape, scale):
        return jax.random.normal(kk, shape, jnp.float32) * scale

    def gain(kk, n):
        return 1.0 + 0.01 * jax.random.normal(kk, (L, n), jnp.float32)

    return {
        "x": nrm(k[0], (BATCH, SEQ, D_MODEL), 1.0),
        "norm_gain": gain(k[1], D_MODEL),
        "w_in": nrm(k[2], (L, D_MODEL, IN_COLS), D_MODEL ** -0.5),
        "nsa_q_gain": gain(k[3], Dh),
        "nsa_kc_gain": gain(k[4], Dh),
        "nsa_ks_gain": gain(k[5], Dh),
        "nsa_kw_gain": gain(k[6], Dh),
        "cmp_pe_k": nrm(k[7], (L, CMP_BLOCK, Dh), 0.1),
        "cmp_k_w1": nrm(k[8], (L, CMP_BLOCK * Dh, CMP_HIDDEN), (CMP_BLOCK * Dh) ** -0.5),
        "cmp_k_b1": nrm(k[9], (L, CMP_HIDDEN), 0.01),
        "cmp_k_w2": nrm(k[10], (L, CMP_HIDDEN, Dh), CMP_HIDDEN ** -0.5),
        "cmp_pe_v": nrm(k[11], (L, CMP_BLOCK, Dh), 0.1),
        "cmp_v_w1": nrm(k[12], (L, CMP_BLOCK * Dh, CMP_HIDDEN), (CMP_BLOCK * Dh) ** -0.5),
        "cmp_v_b1": nrm(k[13], (L, CMP_HIDDEN), 0.01),
        "cmp_v_w2": nrm(k[14], (L, CMP_HIDDEN, Dh), CMP_HIDDEN ** -0.5),
        "dsa_q_gain": gain(k[15], Dh),
        "dsa_k_gain": gain(k[16], Dh),
        "w_out": nrm(k[17], (L, MIX_WIDTH, D_MODEL), MIX_WIDTH ** -0.5),
    }


def reference(x, norm_gain, w_in, nsa_q_gain, nsa_kc_gain, nsa_ks_gain, nsa_kw_gain,
              cmp_pe_k, cmp_k_w1, cmp_k_b1, cmp_k_w2, cmp_pe_v, cmp_v_w1, cmp_v_b1, cmp_v_w2,
              dsa_q_gain, dsa_k_gain, w_out):
    for l in range(DEPTH):
        x = hybrid_layer(x, norm_gain[l], w_in[l], nsa_q_gain[l], nsa_kc_gain[l], nsa_ks_gain[l],
                         nsa_kw_gain[l], cmp_pe_k[l], cmp_k_w1[l], cmp_k_b1[l], cmp_k_w2[l],
                         cmp_pe_v[l], cmp_v_w1[l], cmp_v_b1[l], cmp_v_w2[l],
                         dsa_q_gain[l], dsa_k_gain[l], w_out[l])
    return x
```

```python
import numpy as np
import ml_dtypes
from contextlib import ExitStack
import concourse.bass as bass
import concourse.mybir as mybir
from concourse.bass_utils import run_bass_kernel_spmd

F32 = mybir.dt.float32
BF16 = mybir.dt.bfloat16
ALU = mybir.AluOpType
AF = mybir.ActivationFunctionType
AX = mybir.AxisListType

ENGS = ['pe', 'act', 'dve', 'pool', 'sp']
NDMA = 12
NSW = 4
S_LEN = 4096
NT = 32
NQ = 16
NEGM = -30000.0
NIT = 15
STEP0 = 64.0
EPS = 1e-6
EARLY_TILES = 6
DEBUG = {}


class Buf:
    __slots__ = ('name', 'w', 'r')

    def __init__(self, name):
        self.name = name
        self.w = None
        self.r = []


class Sched:
    def __init__(self):
        self.ops = {e: [] for e in ENGS}
        self.cnt = {e: 0 for e in ENGS}
        self.seen = {e: {} for e in ENGS}
        self.dma_tgt = [0] * (NDMA + NSW)
        self.dma_next = 0
        self.sw_next = 0
        self.dead = False

    def _wait(self, eng, dep):
        if dep is None:
            return
        key, val = dep
        if key == eng and eng in ('pe', 'sp'):
            return
        if self.seen[eng].get(key, 0) >= val:
            return
        self.seen[eng][key] = val
        self.ops[eng].append(('wait', key, val))

    def op(self, eng, fn, reads=(), writes=(), dma=False):
        if self.dead:
            return None
        for b in reads:
            self._wait(eng, b.w)
        for b in writes:
            self._wait(eng, b.w)
            for d in b.r:
                self._wait(eng, d)
        if eng == 'sp' or dma:
            if dma:
                k = NDMA + self.sw_next
                self.sw_next = (self.sw_next + 1) % NSW
            else:
                k = self.dma_next
                self.dma_next = (k + 1) % NDMA
            key = ('dma', k)
            if self.dma_tgt[k] > 0:
                self._wait(eng, (key, self.dma_tgt[k]))
            self.dma_tgt[k] += 16
            dep = (key, self.dma_tgt[k])
            self.ops[eng].append(('dma', fn, k))
        else:
            self.cnt[eng] += 1
            dep = (eng, self.cnt[eng])
            self.ops[eng].append(('op', fn))
        for b in reads:
            b.r.append(dep)
        for b in writes:
            b.w = dep
            b.r = []
        return dep

    def barrier(self):
        if self.dead:
            return
        deps = [(e, self.cnt[e]) for e in ['pe', 'act', 'dve', 'pool'] if self.cnt[e] > 0]
        deps += [(('dma', k), self.dma_tgt[k]) for k in range(NDMA + NSW) if self.dma_tgt[k] > 0]
        for e in ENGS:
            for d in deps:
                self._wait(e, d)

    def emit(self, block, sems, dma_sems):
        import bisect
        ref = {e: set() for e in ENGS}
        for e in ENGS:
            for item in self.ops[e]:
                if item[0] == 'wait' and not isinstance(item[1], tuple):
                    ref[item[1]].add(item[2])
        refl = {e: sorted(v) for e, v in ref.items()}

        def run(name, eng):
            idx = 0
            for item in self.ops[name]:
                if item[0] == 'wait':
                    key, val = item[1], item[2]
                    if isinstance(key, tuple):
                        eng.wait_ge(dma_sems[key[1]], val)
                    else:
                        eng.wait_ge(sems[key], bisect.bisect_right(refl[key], val))
                elif item[0] == 'op':
                    idx += 1
                    inst = item[1](eng)
                    if idx in ref[name]:
                        inst.then_inc(sems[name], 1)
                else:
                    item[1](eng).then_inc(dma_sems[item[2]], 16)

        @block.tensor
        def _(e):
            run('pe', e)

        @block.scalar
        def _(e):
            run('act', e)

        @block.vector
        def _(e):
            run('dve', e)

        @block.gpsimd
        def _(e):
            run('pool', e)

        @block.sync
        def _(e):
            run('sp', e)
            for k in range(NDMA + NSW):
                if self.dma_tgt[k] > 0:
                    e.wait_ge(dma_sems[k], self.dma_tgt[k])


class _Stop(Exception):
    pass


def build(nq=NQ, nt=NT, dbg=(), stage=99):
    nc = bass.Bass("TRN2", target_bir_lowering=False)
    S = Sched()

    def din(name, shape, dt=F32):
        return nc.dram_tensor(name, list(shape), dt, kind="ExternalInput").ap()

    x_all = din("x_all", [S_LEN, 1024])
    x_own = din("x_own", [NQ * 128, 1024])
    wK_d = din("wK", [128, 8, 1088])
    wQ_d = din("wQ", [128, 8, 2332])
    wO_d = din("wO", [128, 8, 1024])
    ng_d = din("ng", [128, 1024])
    gk_d = din("gk", [128, 384])
    gq_d = din("gq", [128, 3, 64])
    w1_d = din("w1", [2, 128, 32, 256])
    b1_d = din("b1", [128, 4])
    w2_d = din("w2", [128, 2, 2, 64])
    pe_d = din("peT", [128, 2, 32])
    ident_d = din("ident", [128, 128], BF16)
    i4_d = din("i4", [128, 512], BF16)
    mab_d = din("mab", [128, 256], BF16)
    mabf_d = din("mabf", [128, 256])
    win_d = din("winneg", [128, 6, 128], BF16)
    cmpneg_d = din("cmpneg", [NQ * 128, 256], BF16)
    ovl_d = din("ovl", [128, 2, 64], BF16)
    tab_d = din("nsatab", [NQ * 128, 64])
    cs_all_d = din("cs_all", [S_LEN, 128])
    cs_own_d = din("cs_own", [NQ * 128, 128])
    cs_cmp_d = din("cs_cmp", [256, 128])
    y = nc.dram_tensor("y", [NQ * 128, 1024], F32, kind="ExternalOutput").ap()
    dbg_out = {}
    for name, shape in dbg:
        dbg_out[name] = nc.dram_tensor("dbg_" + name, list(shape), F32, kind="ExternalOutput").ap()

    with ExitStack() as es:
        def mark(n):
            if stage == n:
                S.dead = True

        def sb(name, shape, dt=F32):
            return es.enter_context(nc.sbuf_tensor("s_" + name, list(shape), dt))

        def ps(name, shape, dt=F32):
            return es.enter_context(nc.psum_tensor("p_" + name, list(shape), dt))

        def dma(out, in_, reads=(), writes=()):
            S.op('sp', lambda e: e.dma_start(out=out, in_=in_), reads, writes)

        def mm(out, lhsT, rhs, start, stop, reads, writes):
            S.op('pe', lambda e: e.matmul(out, lhsT=lhsT, rhs=rhs, start=start, stop=stop,
                                          skip_group_check=True), reads, writes)

        def tr(out, in_, reads, writes):
            S.op('pe', lambda e: e.transpose(out=out, in_=in_, identity=ident[:]), list(reads) + [b_const], writes)

        def act(out, in_, func, reads, writes, scale=None, bias=None, accum_out=None):
            kw = {}
            if scale is not None:
                kw['scale'] = scale
            if bias is not None:
                kw['bias'] = bias
            if accum_out is not None:
                kw['accum_out'] = accum_out
            S.op('act', lambda e: e.activation(out=out, in_=in_, func=func, **kw), reads, writes)

        def tt(eng, out, in0, in1, op, reads, writes):
            S.op(eng, lambda e: e.tensor_tensor(out=out, in0=in0, in1=in1, op=op), reads, writes)

        def ts(eng, out, in0, s1, s2, op0, op1, reads, writes, accum_out=None):
            if op1 is None:
                S.op(eng, lambda e: e.tensor_scalar(out=out, in0=in0, scalar1=s1, scalar2=None, op0=op0), reads, writes)
            elif accum_out is not None:
                S.op(eng, lambda e: e.tensor_scalar(out=out, in0=in0, scalar1=s1, scalar2=None, op0=op0, op1=op1,
                                                    accum_out=accum_out), reads, writes)
            else:
                S.op(eng, lambda e: e.tensor_scalar(out=out, in0=in0, scalar1=s1, scalar2=s2, op0=op0, op1=op1),
                     reads, writes)

        def stt(out, in0, scalar, in1, op0, op1, reads, writes):
            S.op('dve', lambda e: e.scalar_tensor_tensor(out=out, in0=in0, scalar=scalar, in1=in1, op0=op0, op1=op1),
                 reads, writes)

        def cp(eng, out, in_, reads, writes):
            S.op(eng, lambda e: e.tensor_copy(out=out, in_=in_), reads, writes)

        def red(out, in_, op, reads, writes):
            S.op('dve', lambda e: e.tensor_reduce(out=out, in_=in_, axis=AX.X, op=op), reads, writes)

        def recip(out, in_, reads, writes):
            S.op('dve', lambda e: e.reciprocal(out=out, in_=in_), reads, writes)

        def memset(eng, ap, val, writes):
            S.op(eng, lambda e: e.memset(ap, val), (), writes)

        def dump(name, src_ap, b):
            if name in dbg_out:
                dma(dbg_out[name], src_ap, reads=[b])

        b_const = Buf("const")
        ident = sb("ident", [128, 128], BF16)
        i4 = sb("i4", [128, 512], BF16)
        mab = sb("mab", [128, 256], BF16)
        mabf = sb("mabf", [128, 256])
        winneg = sb("winneg", [128, 6, 128], BF16)
        ngain = sb("ngain", [128, 1024])
        gq = sb("gq", [128, 3, 64])
        KT = sb("KT", [128, 4, S_LEN], BF16)
        b_KT = [Buf("KT%d" % t) for t in range(NT)]
        Vaug = sb("Vaug", [128, NT, 3, 2, 65], BF16)
        b_V = [Buf("V%d" % t) for t in range(NT)]
        KcT = sb("KcT", [128, 256], BF16)
        b_KcT = Buf("KcT")
        VcA = sb("VcA", [128, 2, 2, 128], BF16)
        b_VcA = Buf("VcA")
        epsb = sb("epsb", [128, 1])
        wQb = sb("wQb", [128, 8, 2332], BF16)
        b_wQ = Buf("wQb")
        wOb = sb("wOb", [128, 8, 1024], BF16)
        b_wO = Buf("wOb")

        PB = [ps("pb%d" % k, [128, 512]) for k in range(6)]
        b_PB = [Buf("pb%d" % k) for k in range(6)]
        T0 = ps("t0", [128, 8, 128], BF16)
        b_T0 = Buf("t0")
        T1 = ps("t1", [128, 8, 128], BF16)
        b_T1 = Buf("t1")

        SCR = {}

        def alloc_scratch(sbf, nb, tag):
            SCR['n'] = nb
            SCR['wstw'] = 1200 if nb == 2 else 600
            for nm, shape, dt in [("xt", [128, 1024], F32), ("xn", [128, 1024], BF16), ("hT", [128, 8, 128], BF16),
                                  ("st1", [128, 4], F32), ("cs", [128, 128], F32), ("kbuf", [128, 8, 64], F32),
                                  ("ktmp", [128, 8, 64], F32), ("ktmp2", [128, 8, 64], F32), ("kst", [128, 8], F32),
                                  ("kst2", [128, 8], F32), ("krot", [128, 8, 64], BF16)]:
                nbb = 1 if nm in ("ktmp", "ktmp2") else nb
                SCR[nm] = [sbf("%s%s%d" % (nm, tag, k), shape, dt) for k in range(nbb)]
                SCR["b_" + nm] = [Buf("%s%s%d" % (nm, tag, k)) for k in range(nbb)]
        wst_n = [0]

        dma(ident[:], ident_d[:, :], writes=[b_const])
        dma(i4[:], i4_d[:, :], writes=[b_const])
        dma(mab[:], mab_d[:, :], writes=[b_const])
        dma(mabf[:], mabf_d[:, :], writes=[b_const])
        dma(winneg[:], win_d[:, :, :], writes=[b_const])
        dma(ngain[:], ng_d[:, :], writes=[b_const])
        dma(gq[:], gq_d[:, :, :], writes=[b_const])
        memset('pool', epsb[:], EPS, [b_const])
        memset('pool', Vaug[:, :, :, :, 64:65], 1.0, b_V)

        WT = {'tasks': [], 'n': 0}

        def load_weight_bf16(dst_fn, src_fn, ncols_total, b_dst, nrows=8):
            nch = (ncols_total + 599) // 600
            cw = (ncols_total + nch - 1) // nch
            for r in range(nrows):
                for c in range(nch):
                    c0, c1 = c * cw, min((c + 1) * cw, ncols_total)
                    WT['tasks'].append((dst_fn(r, c0, c1), src_fn(r, c0, c1), c1 - c0, b_dst))

        def pump(n):
            for _ in range(n):
                if not WT['tasks']:
                    return
                dst, src, w, b_dst = WT['tasks'].pop(0)
                k = WT['n'] % 2
                WT['n'] += 1
                dma(WT['wst'][k][:, 0:w], src, writes=[WT['b_wst'][k]])
                cp('dve', dst, WT['wst'][k][:, 0:w], [WT['b_wst'][k]], [b_dst])

        WT['pend'] = []

        def pump_cast():
            for (dst, k, w, b_dst) in WT['pend']:
                cp('dve', dst, WT['wst'][k][:, 0:w], [WT['b_wst'][k]], [b_dst])
            WT['pend'] = []

        def pump_dma(n):
            for _ in range(n):
                if not WT['tasks'] or len(WT['pend']) >= 2:
                    return
                dst, src, w, b_dst = WT['tasks'].pop(0)
                k = WT['n'] % 2
                WT['n'] += 1
                dma(WT['wst'][k][:, 0:w], src, writes=[WT['b_wst'][k]])
                WT['pend'].append((dst, k, w, b_dst))

        def rms_to_hT(src_dram_tile, k):
            dma(xt[k][:], src_dram_tile, writes=[b_xt[k]])
            act(xn[k][:], xt[k][:], AF.Square, [b_xt[k]], [b_xn[k], b_st[k]], accum_out=st1[k][:, 0:1])
            act(st1[k][:, 1:2], st1[k][:, 0:1], AF.Sqrt, [b_st[k], b_const], [b_st[k]], scale=1.0 / 1024.0, bias=epsb[:, 0:1])
            recip(st1[k][:, 2:3], st1[k][:, 1:2], [b_st[k]], [b_st[k]])
            stt(xn[k][:], xt[k][:], st1[k][:, 2:3], ngain[:], ALU.mult, ALU.mult, [b_xt[k], b_st[k], b_const], [b_xn[k]])
            for c in range(8):
                tr(T0[:, c, :], xn[k][:, c * 128:(c + 1) * 128], [b_xn[k]], [b_T0])
            act(hT[k][:], T0[:], AF.Copy, [b_T0], [b_hT[k]])

        def norm_rope(k, nh_norm, nh, gain_ap, out_ap, b_out, extra_reads=(), kb=None, b_kb=None, pre_sd=None, b_pre=None):
            if kb is None:
                kb = kbuf[k]
                b_kb = b_kbuf[k]
            if nh_norm > 0 and pre_sd is not None:
                tt('dve', kb[:, 0:nh_norm, :], kb[:, 0:nh_norm, :],
                   pre_sd.unsqueeze(2).to_broadcast([128, nh_norm, 64]), ALU.mult,
                   [b_kb, b_pre], [b_kb])
                tt('dve', kb[:, 0:nh_norm, :], kb[:, 0:nh_norm, :], gain_ap, ALU.mult,
                   [b_kb, b_const], [b_kb])
            elif nh_norm > 0:
                tt('dve', ktmp[0][:, 0:nh_norm, :], kb[:, 0:nh_norm, :], kb[:, 0:nh_norm, :], ALU.mult,
                   [b_kb], [b_ktmp[0]])
                red(kst[k][:, 0:nh_norm], ktmp[0][:, 0:nh_norm, :], ALU.add, [b_ktmp[0]], [b_kst[k]])
                act(kst2[k][:, 0:nh_norm], kst[k][:, 0:nh_norm], AF.Sqrt, [b_kst[k], b_const], [b_kst[k]],
                    scale=1.0 / 64.0, bias=epsb[:, 0:1])
                recip(kst[k][:, 0:nh_norm], kst2[k][:, 0:nh_norm], [b_kst[k]], [b_kst[k]])
                tt('dve', kb[:, 0:nh_norm, :], kb[:, 0:nh_norm, :],
                   kst[k][:, 0:nh_norm].unsqueeze(2).to_broadcast([128, nh_norm, 64]), ALU.mult,
                   [b_kb, b_kst[k]], [b_kb])
                tt('dve', kb[:, 0:nh_norm, :], kb[:, 0:nh_norm, :], gain_ap, ALU.mult,
                   [b_kb, b_const], [b_kb])
            cC = cs[k][:, 0:64].unsqueeze(1).to_broadcast([128, nh, 64])
            sA = cs[k][:, 64:96].unsqueeze(1).to_broadcast([128, nh, 32])
            sB = cs[k][:, 96:128].unsqueeze(1).to_broadcast([128, nh, 32])
            tt('dve', ktmp[0][:, 0:nh, :], kb[:, 0:nh, :], cC, ALU.mult, [b_kb, b_cs[k]], [b_ktmp[0]])
            tt('pool', ktmp2[0][:, 0:nh, 0:32], kb[:, 0:nh, 32:64], sA, ALU.mult, [b_kb, b_cs[k]], [b_ktmp2[0]])
            tt('pool', ktmp2[0][:, 0:nh, 32:64], kb[:, 0:nh, 0:32], sB, ALU.mult, [b_kb, b_cs[k]], [b_ktmp2[0]])
            tt('dve', out_ap, ktmp[0][:, 0:nh, :], ktmp2[0][:, 0:nh, :], ALU.add,
               [b_ktmp[0], b_ktmp2[0]] + list(extra_reads), [b_out])

        try:
            with ExitStack() as es1:
                def sb1(name, shape, dt=F32):
                    return es1.enter_context(nc.sbuf_tensor("s_" + name, list(shape), dt))
                alloc_scratch(sb1, 2, "a")
                xt, xn, hT, st1, cs, kbuf, ktmp, ktmp2, kst, kst2, krot = (SCR[n_] for n_ in
                    ["xt", "xn", "hT", "st1", "cs", "kbuf", "ktmp", "ktmp2", "kst", "kst2", "krot"])
                b_xt, b_xn, b_hT, b_st, b_cs, b_kbuf, b_ktmp, b_ktmp2, b_kst, b_krot = (SCR["b_" + n_] for n_ in
                    ["xt", "xn", "hT", "st1", "cs", "kbuf", "ktmp", "ktmp2", "kst", "krot"])
                WT['wst'] = [sb1("wst%d" % k, [128, 600]) for k in range(2)]
                WT['b_wst'] = [Buf("wst%d" % k) for k in range(2)]
                gk = sb1("gk", [128, 6, 64])
                dma(gk[:], gk_d.rearrange("p (a b) -> p a b", b=64), writes=[b_const])
                wKb = sb1("wKb", [128, 8, 1088], BF16)
                b_wK = Buf("wKb")
                rawT = sb1("rawT", [128, 2, S_LEN], BF16)
                b_raw = Buf("rawT")
                w1b1 = sb1("w1b", [128, 32, 256], BF16)
                b_w11 = Buf("w1b")
                kss = [sb1("kss%d" % k, [128, 16]) for k in range(2)]
                b_kss = [Buf("kss%d" % k) for k in range(2)]
                sqj = [sb1("sqj%d" % k, [128, 64], BF16) for k in range(2)]
                kvraw = [sb1("kvraw%d" % k, [128, 256], BF16) for k in range(2)]
                b_kvraw = [Buf("kvraw%d" % k) for k in range(2)]
                hidT = sb1("hidT", [128, 2, 256], BF16)
                b_hid = Buf("hidT")
                cbias = sb1("cbias", [128, 4])
                b_cb = Buf("cbias")
                b1s = sb1("b1s", [128, 4])
                w2f = sb1("w2f", [128, 2, 2, 64])
                w2b = sb1("w2b", [128, 2, 2, 64], BF16)
                pef = sb1("pef", [128, 2, 32])
                peb = sb1("peb", [128, 2, 32], BF16)
                ovl = sb1("ovl", [128, 2, 64], BF16)
                b_c1 = Buf("c1")

                w1v = wKb[:].rearrange("p a b -> p (a b)")[:, 0:8192].rearrange("p (l j) -> p l j", j=256)
                w1b = [w1b1, w1v]
                b_w1 = [b_w11, b_wK]
                stgA = w1b1[:].rearrange("p a b -> p (a b)").bitcast(F32)
                stgB = rawT[:].rearrange("p a b -> p (a b)").bitcast(F32)
                for stg, b_stg, r0 in ((stgA, b_w11, 0), (stgB, b_raw, 3)):
                    dma(stg[:, 0:3264].rearrange("p (a b) -> p a b", b=1088), wK_d[:, r0:r0 + 3, :], writes=[b_stg])
                load_weight_bf16(lambda r, c0, c1: wKb[:, 6 + r, c0:c1], lambda r, c0, c1: wK_d[:, 6 + r, c0:c1], 1088, b_wK, nrows=2)
                pump(1000)
                for stg, b_stg, r0 in ((stgA, b_w11, 0), (stgB, b_raw, 3)):
                    for r in range(3):
                        cp('dve' if r != 1 else 'pool', wKb[:, r0 + r, :], stg[:, r * 1088:(r + 1) * 1088], [b_stg], [b_wK])

                def load_w1(kv):
                    load_weight_bf16(lambda r, c0, c1: w1b[kv][:, r * 4:(r + 1) * 4, :].rearrange("p a b -> p (a b)")[:, c0:c1],
                                     lambda r, c0, c1: w1_d[kv, :, r * 4:(r + 1) * 4, :].rearrange("p a b -> p (a b)")[:, c0:c1],
                                     1024, b_w1[kv], nrows=8)
                dma(b1s[:], b1_d[:, :], writes=[b_c1])
                dma(w2f[:], w2_d[:, :, :, :], writes=[b_c1])
                dma(pef[:], pe_d[:, :, :], writes=[b_c1])
                dma(ovl[:], ovl_d[:, :, :], writes=[b_c1])
                cp('pool', w2b[:], w2f[:], [b_c1], [b_c1])
                cp('pool', peb[:], pef[:], [b_c1], [b_c1])

                mark(1)
                b_T1a = b_T1

                def F1(t):
                    k = t % 2
                    dma(xt[k][:], x_all[t * 128:(t + 1) * 128, :], writes=[b_xt[k]])
                    act(xn[k][:], xt[k][:], AF.Square, [b_xt[k]], [b_xn[k], b_st[k]], accum_out=st1[k][:, 0:1])
                    act(st1[k][:, 1:2], st1[k][:, 0:1], AF.Sqrt, [b_st[k], b_const], [b_st[k]], scale=1.0 / 1024.0, bias=epsb[:, 0:1])
                    recip(st1[k][:, 2:3], st1[k][:, 1:2], [b_st[k]], [b_st[k]])
                    stt(xn[k][:], xt[k][:], st1[k][:, 2:3], ngain[:], ALU.mult, ALU.mult, [b_xt[k], b_st[k], b_const], [b_xn[k]])

                def F2(t):
                    k = t % 2
                    for c in range(8):
                        tr(T0[:, c, :], xn[k][:, c * 128:(c + 1) * 128], [b_xn[k]], [b_T0])
                    act(hT[k][:], T0[:], AF.Copy, [b_T0], [b_hT[k]])

                def M(t):
                    k = t % 2
                    bA, bB, bC = (0, 1, 2) if k == 0 else (3, 4, 5)
                    for kc in range(8):
                        mm(PB[bA][:, 0:448], hT[k][:, kc, :], wKb[:, kc, 0:448], kc == 0, kc == 7, [b_hT[k], b_wK], [b_PB[bA]])
                    for kc in range(8):
                        mm(PB[bB][:, 0:256], hT[k][:, kc, :], wKb[:, kc, 448:704], kc == 0, kc == 7, [b_hT[k], b_wK], [b_PB[bB]])
                    for kc in range(8):
                        mm(PB[bC][:, 0:384], hT[k][:, kc, :], wKb[:, kc, 704:1088], kc == 0, kc == 7, [b_hT[k], b_wK], [b_PB[bC]])
                    act(kvraw[k][:], PB[bB][:, 0:256], AF.Copy, [b_PB[bB]], [b_kvraw[k]])
                    act(kbuf[k][:, 0:7, :], PB[bA][:, 0:448].rearrange("p (a b) -> p a b", b=64), AF.Copy, [b_PB[bA]], [b_kbuf[k]])
                    act(kbuf[k][:, 7, :], PB[bA][:, 384:448], AF.Copy, [b_PB[bA]], [b_kbuf[k]])
                    act(Vaug[:, t, :, :, 0:64], PB[bC][:, 0:384].rearrange("p (a b c) -> p a b c", a=3, b=2), AF.Copy,
                        [b_PB[bC]], [b_V[t]])
                    for j in range(2):
                        tr(T1[:, 4 + j, :], kvraw[k][:, j * 128:(j + 1) * 128], [b_kvraw[k]], [b_T1a])
                    act(rawT[:].rearrange("p k (j c) -> p k c j", j=16)[:, :, t * 8:(t + 1) * 8, :],
                        T1[:, 4:6, :].rearrange("p k (c j) -> p k c j", j=16), AF.Copy, [b_T1a], [b_raw])

                def B1(t):
                    k = t % 2
                    norm_rope(k, 6, 8, gk[:, :, :], krot[k][:], b_krot[k])

                def B2(t):
                    k = t % 2
                    for j in range(4):
                        tr(T1[:, j, :], krot[k][:, 2 * j:2 * j + 2, :].rearrange("p a b -> p (a b)"), [b_krot[k]], [b_T1])
                    act(KT[:, :, t * 128:(t + 1) * 128], T1[:, 0:4, :], AF.Copy, [b_T1], [b_KT[t]])

                def load_cs(t):
                    dma(cs[t % 2][:], cs_all_d[t * 128:(t + 1) * 128, :], writes=[b_cs[t % 2]])

                F1(0)
                load_cs(0)
                F2(0)
                if nt > 1:
                    F1(1)
                for t in range(nt):
                    if t + 2 < nt:
                        F1(t + 2)
                    pump_cast()
                    if t + 1 < nt:
                        F2(t + 1)
                    M(t)
                    B1(t)
                    if t + 1 < nt:
                        load_cs(t + 1)
                    if t >= 1:
                        B2(t - 1)
                    if t == 0:
                        load_w1(0)
                        load_weight_bf16(lambda r, c0, c1: wQb[:, r, c0:c1], lambda r, c0, c1: wQ_d[:, r, c0:c1], 2332, b_wQ)
                        load_weight_bf16(lambda r, c0, c1: wOb[:, r, c0:c1], lambda r, c0, c1: wO_d[:, r, c0:c1], 1024, b_wO)
                    pump_dma(2)
                B2(nt - 1)
                pump_cast()
                pump(1000)
                mark(3)
                memset('pool', hidT[:], 0.0, [b_hid])
                if nt < NT:
                    memset('pool', rawT[:].rearrange("p k (j c) -> p k j c", j=16)[:, :, :, nt * 8:], 0.0, [b_raw])
                for g in range(2):
                    cp('pool', VcA[:, :, g, 64:128], ovl[:], [b_c1], [b_VcA])
                ck = 0
                dma(cs[ck][:], cs_cmp_d[0:128, :], writes=[b_cs[ck]])
                dma(cs[1][:], cs_cmp_d[128:256, :], writes=[b_cs[1]])
                for kv in range(2):
                    for hh in range(2):
                        bk = 0
                        for l in range(32):
                            mm(PB[bk][:, 0:1], w1b[kv][0:64, l, hh * 128:(hh + 1) * 128], peb[0:64, kv, l:l + 1],
                               l == 0, l == 31, [b_w1[kv], b_c1], [b_PB[bk]])
                        tt('dve', cbias[:, kv * 2 + hh:kv * 2 + hh + 1], PB[bk][:, 0:1], b1s[:, kv * 2 + hh:kv * 2 + hh + 1],
                           ALU.add, [b_PB[bk], b_c1], [b_cb])
                    if kv == 0:
                        load_w1(1)
                        pump(1000)
                    for g in range(2):
                        gs = slice(g * 64, (g + 1) * 64)
                        for hh in range(2):
                            bk = 1 + hh
                            for l in range(32):
                                mm(PB[bk][:, 0:255], w1b[kv][gs, l, hh * 128:(hh + 1) * 128],
                                   rawT[gs, kv, (l % 16) * 256 + l // 16:(l % 16) * 256 + l // 16 + 255], l == 0, l == 31,
                                   [b_w1[kv], b_raw], [b_PB[bk]])
                            act(hidT[:, hh, 0:255], PB[bk][:, 0:255], AF.Silu, [b_PB[bk], b_cb], [b_hid],
                                bias=cbias[:, kv * 2 + hh:kv * 2 + hh + 1])
                        for ct in range(2):
                            bk = 3 + ct
                            for hh in range(2):
                                mm(PB[bk][:, 0:64], hidT[:, hh, ct * 128:(ct + 1) * 128], w2b[:, kv, hh, :],
                                   hh == 0, hh == 1, [b_hid, b_c1], [b_PB[bk]])
                            if kv == 0:
                                act(kbuf[ct][:, g, :], PB[bk][:, 0:64], AF.Copy, [b_PB[bk]], [b_kbuf[ct]])
                            else:
                                act(VcA[:, ct, g, 0:64], PB[bk][:, 0:64], AF.Copy, [b_PB[bk]], [b_VcA])
                    if kv == 0:
                        for ct in range(2):
                            norm_rope(ct, 2, 2, gq[:, 2:3, :].to_broadcast([128, 2, 64]), krot[ct][:, 0:2, :], b_krot[ct])
                            tr(T1[:, ct, :], krot[ct][:, 0:2, :].rearrange("p a b -> p (a b)"), [b_krot[ct]], [b_T1])
                        act(KcT[:].rearrange("p (a b) -> p a b", b=128), T1[:, 0:2, :], AF.Copy, [b_T1], [b_KcT])
                mark(4)
                S.barrier()

            if "KT" in dbg_out:
                for j in range(4):
                    for t in range(nt):
                        pass
            dbgst = sb("dbgst", [128, 1024]) if dbg_out else None
            b_dbgst = Buf("dbgst")

            def dump_bf(name, src_ap, n, b):
                if name in dbg_out:
                    cp('dve', dbgst[:, 0:n], src_ap, [b], [b_dbgst])
                    dma(dbg_out[name], dbgst[:, 0:n], reads=[b_dbgst])

            for j in range(4):
                for c in range(4):
                    if "KT%d_%d" % (j, c) in dbg_out:
                        cp('dve', dbgst[:, :], KT[:, j, c * 1024:(c + 1) * 1024], b_KT, [b_dbgst])
                        dma(dbg_out["KT%d_%d" % (j, c)], dbgst[:, :], reads=[b_dbgst])
            dump_bf("KcT", KcT[:], 256, b_KcT)
            dump_bf("VcA", VcA[:].rearrange("p a b c -> p (a b c)"), 512, b_VcA)
            dump_bf("V0", Vaug[:, 0, :, :, :].rearrange("p a b c -> p (a b c)"), 390, b_V[0])

            with ExitStack() as es2:
                def sb2(name, shape, dt=F32):
                    return es2.enter_context(nc.sbuf_tensor("s_" + name, list(shape), dt))
                SC = sb2("SC", [128, S_LEN])
                b_SC = Buf("SC")
                SCR['wstw'] = 600
                for nm, shape, dt in [("xn", [128, 1024], BF16), ("hT", [128, 8, 128], BF16),
                                      ("st1", [128, 4], F32), ("cs", [128, 128], F32),
                                      ("ktmp", [128, 8, 64], F32), ("ktmp2", [128, 8, 64], F32), ("kst", [128, 8], F32),
                                      ("kst2", [128, 8], F32)]:
                    SCR[nm] = [sb2("%sb" % nm, shape, dt)]
                    SCR["b_" + nm] = [Buf("%sb" % nm)]
                SCR["xt"] = [sb2("xtb%d" % k, [128, 1024]) for k in range(2)]
                SCR["b_xt"] = [Buf("xtb%d" % k) for k in range(2)]
                kq = [sb2("kq%d" % j, [128, 8, 64]) for j in range(2)] + [sb2("kq2", [128, 4, 64])]
                b_kq = [Buf("kq%d" % j) for j in range(3)]
                xt, xn, hT, st1, cs, ktmp, ktmp2, kst, kst2 = (SCR[n_] for n_ in
                    ["xt", "xn", "hT", "st1", "cs", "ktmp", "ktmp2", "kst", "kst2"])
                b_xt, b_xn, b_hT, b_st, b_cs, b_ktmp, b_ktmp2, b_kst = (SCR["b_" + n_] for n_ in
                    ["xt", "xn", "hT", "st1", "cs", "ktmp", "ktmp2", "kst"])
                QnT = [[sb2("QnT%d_%d" % (k, g), [128, 4, 128], BF16) for g in range(2)] for k in range(2)]
                QdT = [[sb2("QdT%d_%d" % (k, g), [128, 4, 128], BF16) for g in range(2)] for k in range(2)]
                b_QnT = [Buf("QnT%d" % k) for k in range(2)]
                b_QdT = [Buf("QdT%d" % k) for k in range(2)]
                QiT = sb2("QiT", [128, 4, 128], BF16)
                b_QiT = Buf("QiT")
                for k in range(2):
                    for g in range(2):
                        memset('pool', QnT[k][g][:], 0.0, [b_QnT[k]])
                        memset('pool', QdT[k][g][:], 0.0, [b_QdT[k]])
                memset('pool', QiT[:], 0.0, [b_QiT])
                Zs = [sb2("Zs%d" % k, [128, 1024], BF16) for k in range(2)]
                b_Zs = [Buf("Zs%d" % k) for k in range(2)]
                G = [sb2("G%d" % k, [128, 24]) for k in range(2)]
                b_G = [Buf("G%d" % k) for k in range(2)]
                wi = sb2("wi", [128, 12])
                b_wi = Buf("wi")
                qst = sb2("qst", [128, 2, 16])
                b_qst = [Buf("qst0"), Buf("qst1")]
                qrot = sb2("qrot", [128, 4, 2, 64], BF16)
                b_qrot = Buf("qrot")
                qif = kq[2]
                b_qif = b_kq[2]
                negd = sb2("negd", [128, S_LEN], BF16)
                b_negd = Buf("negd")
                junk8 = sb2("junk8", [128, S_LEN], mybir.dt.uint8)
                b_junk8 = Buf("junk8")
                negs = sb2("negs", [128, S_LEN], BF16)
                b_negs = Buf("negs")
                NET = 2
                ET = [sb2("ET%d" % k, [128, 512], BF16) for k in range(NET)]
                b_ET = [Buf("ET%d" % k) for k in range(NET)]
                R = [ET[0], ET[1], ET[0], ET[1]]
                b_R = [b_ET[0], b_ET[1], b_ET[0], b_ET[1]]
                et_n = [0]
                cmpneg = [sb2("cmpneg%d" % k, [128, 256], BF16) for k in range(2)]
                b_cmpneg = [Buf("cmpneg%d" % k) for k in range(2)]
                tab = [sb2("tab%d" % k, [128, 64]) for k in range(2)]
                b_tab = [Buf("tab%d" % k) for k in range(2)]
                OcmpA = sb2("OcmpA", [128, 8, 128])
                OselA = sb2("OselA", [128, 8, 65])
                OwinA = sb2("OwinA", [128, 8, 65])
                OdsaA = sb2("OdsaA", [128, 8, 65])
                Ocmp = [OcmpA[:, g * 4:(g + 1) * 4, :] for g in range(2)]
                Osel = [OselA[:, g * 4:(g + 1) * 4, :] for g in range(2)]
                Owin = [OwinA[:, g * 4:(g + 1) * 4, :] for g in range(2)]
                Odsa = [OdsaA[:, g * 4:(g + 1) * 4, :] for g in range(2)]
                b_Ocmp = [Buf("Ocmp%d" % g) for g in range(2)]
                b_Osel = [Buf("Osel%d" % g) for g in range(2)]
                b_Owin = [Buf("Owin%d" % g) for g in range(2)]
                b_Odsa = [Buf("Odsa%d" % g) for g in range(2)]
                sm = sb2("sm", [128, 16])
                b_sm = Buf("sm")
                imp = sb2("imp", [128, 64])
                b_imp = Buf("imp")
                m8 = sb2("m8", [128, 16])
                b_m8 = Buf("m8")
                thr = sb2("thr", [128, 4])
                b_thr = Buf("thr")
                tmpA = sb2("tmpA", [128, 4, 64])
                tmpB = sb2("tmpB", [128, 4, 64])
                b_tmpA, b_tmpB = Buf("tmpA"), Buf("tmpB")
                impn, b_impn = tmpA, b_tmpA
                imp2 = tmpB[:, 0, :]
                nsel = [tmpB[:, 1 + g, :] for g in range(2)]
                b_nsel = [b_tmpB, b_tmpB]
                coef = sb2("coef", [128, 32])
                b_coef = Buf("coef")
                mixed = kq[0][:].rearrange("p a b -> p (a b)").bitcast(BF16)
                b_mixed = b_kq[0]
                mixT = hT[0]
                b_mixT = b_hT[0]


                o_n = [0]

                def attention(g, kts, kt_mask_fn, Kj, Vj, QT, b_QT, Odst, b_Odst):
                    ob = 2 + (o_n[0] % 2)
                    o_n[0] += 1
                    qflat = QT[g][:, :, :].rearrange("p a b -> p (a b)")
                    nk = len(kts)
                    es_ = [None] * nk
                    for n in range(nk + 1):
                        if n < nk:
                            kt = kts[n]
                            sbk = n % 2
                            masks = kt_mask_fn(kt)
                            mm(PB[sbk][:, :], KT[:, Kj, kt * 128:(kt + 1) * 128], qflat, True, len(masks) == 0,
                               [b_KT[kt], b_QT], [b_PB[sbk]])
                            for mi, (mlhs, mb) in enumerate(masks):
                                mm(PB[sbk][:, :], mlhs, i4[:], False, mi == len(masks) - 1, [b_const] + list(mb), [b_PB[sbk]])
                            e = et_n[0] % NET
                            et_n[0] += 1
                            es_[n] = e
                            act(ET[e][:], PB[sbk][:, :], AF.Exp, [b_PB[sbk]], [b_ET[e]], scale=0.125)
                        if n >= 1:
                            m = n - 1
                            kt = kts[m]
                            e = es_[m]
                            for r in range(4):
                                mm(PB[ob][:, r * 65:(r + 1) * 65], ET[e][:, r * 128:(r + 1) * 128], Vj(kt),
                                   m == 0 and r == 0, m == nk - 1 and r == 3, [b_ET[e], b_V[kt]], [b_PB[ob]])
                    act(Odst, PB[ob][:, 0:260].rearrange("p (a b) -> p a b", b=65), AF.Copy, [b_PB[ob]], [b_Odst])

                def prep_x_front(i):
                    pp = i % 2
                    px = 0
                    k = 0
                    dma(xt[px][:], x_own[i * 128:(i + 1) * 128, :], writes=[b_xt[px]])
                    dma(cs[k][:], cs_own_d[i * 128:(i + 1) * 128, :], writes=[b_cs[k]])
                    dma(cmpneg[pp][:], cmpneg_d[i * 128:(i + 1) * 128, :], writes=[b_cmpneg[pp]])
                    dma(tab[pp][:], tab_d[i * 128:(i + 1) * 128, :], writes=[b_tab[pp]])
                    act(xn[k][:], xt[px][:], AF.Square, [b_xt[px]], [b_xn[k], b_st[k]], accum_out=st1[k][:, 0:1])
                    act(st1[k][:, 1:2], st1[k][:, 0:1], AF.Ln, [b_st[k], b_const], [b_st[k]], scale=1.0 / 1024.0, bias=epsb[:, 0:1])
                    act(st1[k][:, 2:3], st1[k][:, 1:2], AF.Exp, [b_st[k]], [b_st[k]], scale=-0.5)
                    stt(xn[k][:], xt[px][:], st1[k][:, 2:3], ngain[:], ALU.mult, ALU.mult, [b_xt[px], b_st[k], b_const], [b_xn[k]])

                def prep_x_back(i):
                    k = 0
                    for c in range(8):
                        tr(T0[:, c, :], xn[k][:, c * 128:(c + 1) * 128], [b_xn[k]], [b_T0])
                    act(hT[k][:], T0[:], AF.Copy, [b_T0], [b_hT[k]])

                def qproj(bk, c0, n):
                    for kc in range(8):
                        mm(PB[bk][:, 0:n], hT[0][:, kc, :], wQb[:, kc, c0:c0 + n], kc == 0, kc == 7, [b_hT[0], b_wQ], [b_PB[bk]])

                def prep_qi(i):
                    pp = i % 2
                    qproj(4, 2048, 284)
                    act(kq[2][:].rearrange("p a b -> p (a b)"), PB[4][:, 0:256], AF.Copy, [b_PB[4]], [b_kq[2]])
                    act(G[pp][:], PB[4][:, 256:280], AF.Exp, [b_PB[4]], [b_G[pp]], scale=-1.0)
                    ts('dve', G[pp][:], G[pp][:], 1.0, None, ALU.add, None, [b_G[pp]], [b_G[pp]])
                    recip(G[pp][:], G[pp][:], [b_G[pp]], [b_G[pp]])
                    act(wi[:, 0:4], PB[4][:, 280:284], AF.Copy, [b_PB[4]], [b_wi])

                def prep_qproj(i):
                    pp = i % 2
                    for which, (bk, c0) in enumerate([(5, 0), (4, 1024)]):
                        qproj(bk, c0, 512)
                        act(kq[which][:].rearrange("p a b -> p (a b)"), PB[bk][:, :], AF.Copy, [b_PB[bk]], [b_kq[which]])
                        for h in range(8):
                            act(ET[0][:, 0:64], PB[bk][:, h * 64:(h + 1) * 64], AF.Square, [b_PB[bk]], [b_ET[0], b_qst[which]],
                                accum_out=qst[:, which, h:h + 1])
                        act(qst[:, which, 0:8], qst[:, which, 0:8], AF.Ln, [b_qst[which], b_const], [b_qst[which]],
                            scale=1.0 / 64.0, bias=epsb[:, 0:1])
                        act(qst[:, which, 8:16], qst[:, which, 0:8], AF.Exp, [b_qst[which]], [b_qst[which]], scale=-0.5)
                    qproj(5, 512, 512)
                    act(Zs[pp][:, 0:512], PB[5][:, :], AF.Silu, [b_PB[5]], [b_Zs[pp]])
                    qproj(4, 1536, 512)
                    act(Zs[pp][:, 512:1024], PB[4][:, :], AF.Silu, [b_PB[4]], [b_Zs[pp]])

                def qrot_view(which):
                    return xn[0][:, which * 512:(which + 1) * 512].rearrange("p (r g d) -> p r g d", r=4, g=2)

                def prep_qnorm_dve(i):
                    for which in range(2):
                        norm_rope(0, 8, 8, gq[:, which:which + 1, :].to_broadcast([128, 8, 64]),
                                  qrot_view(which).rearrange("p r g d -> p g r d"), b_xn[0], kb=kq[which], b_kb=b_kq[which],
                                  pre_sd=qst[:, which, 8:16], b_pre=b_qst[which])

                def prep_qnorm_pe(i):
                    pp = i % 2
                    for which, (QTt, b_QTt) in enumerate([(QnT[pp], b_QnT[pp]), (QdT[pp], b_QdT[pp])]):
                        qv = qrot_view(which)
                        for r in range(4):
                            tr(T1[:, r, :], qv[:, r, :, :].rearrange("p a b -> p (a b)"), [b_xn[0]], [b_T1])
                        act(QTt[0][0:64, :, :], T1[0:64, 0:4, :], AF.Copy, [b_T1], [b_QTt])
                        act(QTt[1][64:128, :, :], T1[64:128, 0:4, :], AF.Copy, [b_T1], [b_QTt])

                def prep_idx(i):
                    nkeys = (2 * i + 2) * 128
                    k = 0
                    ts('dve', wi[:, 8:12], wi[:, 0:4], 0.0, 2.0, ALU.is_ge, ALU.mult, [b_wi], [b_wi])
                    ts('dve', wi[:, 8:12], wi[:, 8:12], -1.0, None, ALU.add, None, [b_wi], [b_wi])
                    tt('dve', wi[:, 4:8], wi[:, 0:4], wi[:, 8:12], ALU.mult, [b_wi], [b_wi])
                    norm_rope(k, 0, 4, None, kq[2][:], b_kq[2], kb=kq[2], b_kb=b_kq[2])
                    tt('dve', qrot[:].rearrange("p r g d -> p (r g) d")[:, 0:4, :], kq[2][:],
                       wi[:, 4:8].unsqueeze(2).to_broadcast([128, 4, 64]), ALU.mult, [b_kq[2], b_wi], [b_qrot])
                    for j in range(2):
                        tr(T1[:, j, :], qrot[:].rearrange("p r g d -> p (r g d)")[:, j * 128:(j + 1) * 128], [b_qrot], [b_T1])
                    act(QiT[0:64, 0:4:2, :], T1[0:64, 0:2, :], AF.Copy, [b_T1], [b_QiT])
                    act(QiT[64:128, 1:4:2, :], T1[64:128, 0:2, :], AF.Copy, [b_T1], [b_QiT])
                    nch = (nkeys + 511) // 512
                    for ch in range(nch):
                        w = min(512, nkeys - ch * 512)
                        kts_ch = list(range(ch * 4, ch * 4 + w // 128))
                        for h in range(4):
                            bk = 4 + (h % 2)
                            mm(PB[bk][:, 0:w], QiT[:, h, :], KT[:, 3, ch * 512:ch * 512 + w], True, True,
                               [b_QiT] + [b_KT[t] for t in kts_ch], [b_PB[bk]])
                            act(R[h][:, 0:w], PB[bk][:, 0:w], AF.Relu, [b_PB[bk]], [b_R[h]])
                            scs = SC[:, ch * 512:ch * 512 + w]
                            if h == 0:
                                ts('dve', scs, R[0][:, 0:w], wi[:, 8:9], None, ALU.mult, None, [b_R[0], b_wi], [b_SC])
                            else:
                                stt(scs, R[h][:, 0:w], wi[:, 8 + h:9 + h], scs, ALU.mult, ALU.add, [b_R[h], b_wi, b_SC], [b_SC])
                    tt('dve', SC[:, nkeys - 256:nkeys], SC[:, nkeys - 256:nkeys], mabf[:], ALU.add, [b_SC, b_const], [b_SC])

                def search(i):
                    nkeys = (2 * i + 2) * 128
                    if nkeys <= 256:
                        memset('dve', thr[:, 3:4], -1e29, [b_thr])
                        return
                    memset('dve', thr[:, 0:1], 0.0, [b_thr])
                    for it in range(NIT):
                        sn = STEP0 / (2.0 ** it)
                        ts('dve', junk8[:, 0:nkeys], SC[:, 0:nkeys], thr[:, 0:1], 0.0, ALU.is_ge, ALU.add,
                           [b_SC, b_thr], [b_junk8, b_thr], accum_out=thr[:, 1:2])
                        ts('dve', thr[:, 2:3], thr[:, 1:2], 255.5, 2.0 * sn, ALU.is_ge, ALU.mult, [b_thr], [b_thr])
                        stt(thr[:, 0:1], thr[:, 2:3], -sn, thr[:, 0:1], ALU.add, ALU.add, [b_thr], [b_thr])
                    ts('dve', thr[:, 3:4], thr[:, 0:1], -STEP0 / (2.0 ** (NIT - 1)), None, ALU.add, None, [b_thr], [b_thr])

                def search_final(i):
                    nkeys = (2 * i + 2) * 128
                    ts('dve', negd[:, 0:nkeys], SC[:, 0:nkeys], thr[:, 3:4], NEGM, ALU.is_lt, ALU.mult, [b_SC, b_thr], [b_negd])

                def cmp_branch(i, g):
                    pp = i % 2
                    ncts = 2 if i >= 8 else 1
                    ob = 2 + (o_n[0] % 2)
                    o_n[0] += 1
                    qflat = QnT[pp][g][:, :, :].rearrange("p a b -> p (a b)")
                    for ct in range(ncts):
                        sbk = ct % 2
                        mm(PB[sbk][:, :], KcT[:, ct * 128:(ct + 1) * 128], qflat, True, False, [b_KcT, b_QnT[pp]], [b_PB[sbk]])
                        mm(PB[sbk][:, :], cmpneg[pp][:, ct * 128:(ct + 1) * 128], i4[:], False, True, [b_const, b_cmpneg[pp]], [b_PB[sbk]])
                        e = et_n[0] % NET
                        et_n[0] += 1
                        act(ET[e][:], PB[sbk][:, :], AF.Exp, [b_PB[sbk]], [b_ET[e]], scale=0.125)
                        for r in range(4):
                            mm(PB[ob][:, r * 128:(r + 1) * 128], ET[e][:, r * 128:(r + 1) * 128], VcA[:, ct, g, :],
                               ct == 0 and r == 0, ct == ncts - 1 and r == 3, [b_ET[e], b_VcA], [b_PB[ob]])
                    act(Ocmp[g], PB[ob][:, :].rearrange("p (a b) -> p a b", b=128), AF.Copy, [b_PB[ob]], [b_Ocmp[g]])

                def selstats(i, g):
                    pp = i % 2
                    red(sm[:, 0:4], Ocmp[g][:, :, 64:128], ALU.add, [b_Ocmp[g]], [b_sm])
                    ts('dve', sm[:, 0:4], sm[:, 0:4], 1e-30, None, ALU.max, None, [b_sm], [b_sm])
                    recip(sm[:, 4 + 4 * g:8 + 4 * g], sm[:, 0:4], [b_sm], [b_sm])
                    tt('dve', impn[:], Ocmp[g][:, :, 64:128], sm[:, 4 + 4 * g:8 + 4 * g].unsqueeze(2).to_broadcast([128, 4, 64]),
                       ALU.mult, [b_Ocmp[g], b_sm], [b_impn])
                    red(imp[:], impn[:].rearrange("p r j -> p j r"), ALU.add, [b_impn], [b_imp])
                    tt('dve', imp[:], imp[:], tab[pp][:], ALU.add, [b_imp, b_tab[pp]], [b_imp])
                    S.op('dve', lambda e: e.max(out=m8[:, 0:8], in_=imp[:]), [b_imp], [b_m8])
                    S.op('dve', lambda e: e.match_replace(out=imp2, in_to_replace=m8[:, 0:8], in_values=imp[:],
                                                          imm_value=-3.0e38), [b_imp, b_m8], [b_imp, b_tmpB])
                    S.op('dve', lambda e: e.max(out=m8[:, 8:16], in_=imp2), [b_imp, b_tmpB], [b_m8])
                    ts('dve', nsel[g], imp[:], m8[:, 15:16], NEGM, ALU.is_lt, ALU.mult, [b_imp, b_m8], [b_nsel[g]])

                def expand(i, g):
                    nkeys = (2 * i + 2) * 128
                    nblk = nkeys // 64
                    act(negs[:, 0:nkeys].rearrange("p (a b) -> p a b", b=64),
                        nsel[g][:, 0:nblk].unsqueeze(2).to_broadcast([128, nblk, 64]), AF.Copy, [b_nsel[g]], [b_negs])

                def win_branch(i, g):
                    pp = i % 2
                    wkts = [kt for kt in range(2 * i - 4, 2 * i + 2) if kt >= 0]
                    attention(g, wkts, lambda kt: ([] if (kt - (2 * i - 4)) in (2, 3) else [(winneg[:, kt - (2 * i - 4), :], [])]),
                              1, lambda kt: Vaug[:, kt, 1, g, :], QnT[pp], b_QnT[pp], Owin[g], b_Owin[g])

                def sel_branch(i, g):
                    pp = i % 2

                    def selmask(kt):
                        ms = [(negs[:, kt * 128:(kt + 1) * 128], [b_negs])]
                        if kt == 2 * i:
                            ms.append((mab[:, 0:128], []))
                        if kt == 2 * i + 1:
                            ms.append((mab[:, 128:256], []))
                        return ms
                    attention(g, list(range(2 * i + 2)), selmask,
                              0, lambda kt: Vaug[:, kt, 0, g, :], QnT[pp], b_QnT[pp], Osel[g], b_Osel[g])

                def dsa_branch(i, g):
                    pp = i % 2
                    attention(g, list(range(2 * i + 2)),
                              lambda kt: [(negd[:, kt * 128:(kt + 1) * 128], [b_negd])],
                              2, lambda kt: Vaug[:, kt, 2, g, :], QdT[pp], b_QdT[pp], Odsa[g], b_Odsa[g])

                def combine_dve(i):
                    pp = i % 2
                    Gv = G[pp][:].rearrange("p (h b) -> p h b", b=3)
                    cA = kq[1]
                    cB = ktmp[0]
                    b_cA, b_cB = b_kq[1], b_ktmp[0]
                    rO = [b_Ocmp[0], b_Ocmp[1]]
                    tt('dve', coef[:, 0:8], sm[:, 4:12], Gv[:, :, 0], ALU.mult, [b_sm, b_G[pp]], [b_coef])
                    recip(coef[:, 8:16], OselA[:, :, 64], b_Osel, [b_coef])
                    tt('dve', coef[:, 8:16], coef[:, 8:16], Gv[:, :, 1], ALU.mult, [b_coef, b_G[pp]], [b_coef])
                    recip(coef[:, 16:24], OwinA[:, :, 64], b_Owin, [b_coef])
                    tt('dve', coef[:, 16:24], coef[:, 16:24], Gv[:, :, 2], ALU.mult, [b_coef, b_G[pp]], [b_coef])
                    recip(coef[:, 24:32], OdsaA[:, :, 64], b_Odsa, [b_coef])

                    def bc(a):
                        return coef[:, a:a + 8].unsqueeze(2).to_broadcast([128, 8, 64])
                    tt('dve', cA[:], OcmpA[:, :, 0:64], bc(0), ALU.mult, rO + [b_coef], [b_cA])
                    tt('dve', cB[:], OselA[:, :, 0:64], bc(8), ALU.mult, b_Osel + [b_coef], [b_cB])
                    tt('dve', cA[:], cA[:], cB[:], ALU.add, [b_cA, b_cB], [b_cA])
                    tt('dve', cB[:], OwinA[:, :, 0:64], bc(16), ALU.mult, b_Owin + [b_coef], [b_cB])
                    tt('dve', cA[:], cA[:], cB[:], ALU.add, [b_cA, b_cB], [b_cA])
                    tt('dve', mixed[:, 0:512], cA[:].rearrange("p a b -> p (a b)"), Zs[pp][:, 0:512], ALU.mult,
                       [b_cA, b_Zs[pp]], [b_mixed])
                    tt('dve', cB[:], OdsaA[:, :, 0:64], bc(24), ALU.mult, b_Odsa + [b_coef], [b_cB])
                    tt('dve', mixed[:, 512:1024], cB[:].rearrange("p a b -> p (a b)"), Zs[pp][:, 512:1024], ALU.mult,
                       [b_cB, b_Zs[pp]], [b_mixed])

                def outproj(i):
                    pp = 1
                    dma(xt[1][:], x_own[i * 128:(i + 1) * 128, :], writes=[b_xt[1]])
                    for c in range(8):
                        tr(T0[:, c, :], mixed[:, c * 128:(c + 1) * 128], [b_mixed], [b_T0])
                    act(mixT[:], T0[:], AF.Copy, [b_T0], [b_mixT])
                    for nh in range(2):
                        bk = 4 + nh
                        for kc in range(8):
                            mm(PB[bk][:, :], mixT[:, kc, :], wOb[:, kc, nh * 512:(nh + 1) * 512], kc == 0, kc == 7,
                               [b_mixT, b_wO], [b_PB[bk]])
                        tt('dve', xt[pp][:, nh * 512:(nh + 1) * 512], PB[bk][:, :], xt[pp][:, nh * 512:(nh + 1) * 512], ALU.add,
                           [b_PB[bk], b_xt[pp]], [b_xt[pp]])
                    dma(y[i * 128:(i + 1) * 128, :], xt[pp][:], reads=[b_xt[pp]])

                prep_x_front(0)
                prep_x_back(0)
                prep_qi(0)
                prep_idx(0)
                search(0)
                search_final(0)
                prep_qproj(0)
                prep_qnorm_dve(0)
                prep_qnorm_pe(0)
                if nq > 1:
                    prep_x_front(1)
                cmp_branch(0, 0)
                cmp_branch(0, 1)
                win_branch(0, 0)
                win_branch(0, 1)
                for i in range(nq):
                    nxt = i + 1 < nq
                    if nxt:
                        prep_x_back(i + 1)
                        prep_qi(i + 1)
                        prep_qproj(i + 1)
                        prep_idx(i + 1)
                    selstats(i, 0)
                    expand(i, 0)
                    selstats(i, 1)
                    late_search = nxt and i < EARLY_TILES
                    if nxt:
                        prep_qnorm_dve(i + 1)
                        if not late_search:
                            search(i + 1)
                    dsa_branch(i, 0)
                    dsa_branch(i, 1)
                    if nxt and not late_search:
                        search_final(i + 1)
                    sel_branch(i, 0)
                    expand(i, 1)
                    sel_branch(i, 1)
                    if nxt:
                        prep_qnorm_pe(i + 1)
                    combine_dve(i)
                    if i + 2 < nq:
                        prep_x_front(i + 2)
                    if late_search:
                        search(i + 1)
                        search_final(i + 1)
                    if nxt:
                        cmp_branch(i + 1, 0)
                        cmp_branch(i + 1, 1)
                        win_branch(i + 1, 0)
                        win_branch(i + 1, 1)
                    outproj(i)
        except _Stop:
            pass

        sems = {n: es.enter_context(nc.semaphore("s_" + n)) for n in ['pe', 'act', 'dve', 'pool']}
        dma_sems = [es.enter_context(nc.semaphore("d%d" % k)) for k in range(NDMA + NSW)]
        block = es.enter_context(nc.Block())
        S.emit(block, sems, dma_sems)
    return nc


def _consts(p):
    bf = ml_dtypes.bfloat16
    c = {}
    c["ident"] = np.eye(128, dtype=np.float32).astype(bf)
    c["i4"] = np.tile(np.eye(128, dtype=np.float32), (1, 4)).astype(bf)
    q = np.arange(128)[:, None]
    s = np.arange(128)[None, :]
    A = (s <= 128 * p + q)
    B = (128 + s <= 128 * p + q)
    mab = np.concatenate([np.where(A, 0.0, NEGM), np.where(B, 0.0, NEGM)], axis=1).astype(np.float32)
    c["mab"] = mab.astype(bf)
    c["mabf"] = np.where(mab < 0, -1e30, 0.0).astype(np.float32)
    win = np.zeros((128, 6, 128), np.float32)
    for j in range(6):
        diff = 128 * (p + 4 - j) + q - s
        win[:, j, :] = np.where((diff >= 0) & (diff < 512), 0.0, NEGM)
    c["winneg"] = win.astype(bf)
    rows = np.concatenate([np.arange(128 * (2 * i + p), 128 * (2 * i + p) + 128) for i in range(NQ)])
    t = rows[:, None]
    cc = np.arange(256)[None, :]
    c["cmpneg"] = np.where((16 * cc + 31 <= t) & (cc <= 254), 0.0, NEGM).astype(np.float32).astype(bf)
    c_start = np.arange(255) * 16
    j_start = np.arange(64) * 64
    ov = np.clip(np.minimum(c_start[:, None] + 32, j_start[None, :] + 64) - np.maximum(c_start[:, None], j_start[None, :]), 0, None)
    ovl = np.zeros((256, 64), np.float32)
    ovl[:255] = ov.astype(np.float32) / 32.0
    c["ovl"] = np.ascontiguousarray(ovl.reshape(2, 128, 64).transpose(1, 0, 2)).astype(bf)
    j = np.arange(64)[None, :]
    cur = t // 64
    forced = (j == 0) | (j == cur) | (j == cur - 1)
    c["nsatab"] = np.where(forced, 1e6, np.where(j * 64 <= t, 0.0, -1e30)).astype(np.float32)
    half = 32
    inv_freq = (np.float32(10000.0) ** (-np.arange(half, dtype=np.float32) / np.float32(half))).astype(np.float32)

    def cs_table(pos):
        ang = (pos.astype(np.float32)[:, None] * inv_freq[None, :]).astype(np.float32)
        co = np.cos(ang).astype(np.float32)
        si = np.sin(ang).astype(np.float32)
        return np.concatenate([co, co, -si, si], axis=1).astype(np.float32)
    cs_all = cs_table(np.arange(S_LEN))
    c["cs_all"] = cs_all
    c["cs_own"] = np.ascontiguousarray(cs_all[rows])
    cs_cmp = np.zeros((256, 128), np.float32)
    cs_cmp[:255] = cs_table(np.arange(255) * 16 + 31)
    c["cs_cmp"] = cs_cmp
    return c, rows


def _prep_weights(inp):
    f = np.float32
    w_in = np.asarray(inp["w_in"], f)[0]
    colsK = np.concatenate([np.arange(768, 896), np.arange(1024, 1152), np.arange(2328, 2456), np.arange(2840, 2904),
                            np.arange(512, 768), np.arange(896, 1024), np.arange(1152, 1280), np.arange(2456, 2584)])
    colsQ = np.concatenate([np.arange(0, 512), np.arange(1304, 1816), np.arange(1816, 2328), np.arange(2908, 3420),
                            np.arange(2584, 2840), np.arange(1280, 1304), np.arange(2904, 2908)])

    def kmaj(w):
        return np.ascontiguousarray(w.reshape(8, 128, w.shape[1]).transpose(1, 0, 2))
    d = {}
    d["wK"] = kmaj(w_in[:, colsK])
    d["wQ"] = kmaj(w_in[:, colsQ])
    d["wO"] = kmaj(np.asarray(inp["w_out"], f)[0])
    d["ng"] = np.ascontiguousarray(np.broadcast_to(np.asarray(inp["norm_gain"], f)[0][None, :], (128, 1024)))
    gks, gkw, gkd = (np.asarray(inp[n], f)[0] for n in ("nsa_ks_gain", "nsa_kw_gain", "dsa_k_gain"))
    d["gk"] = np.ascontiguousarray(np.broadcast_to(np.concatenate([gks, gks, gkw, gkw, gkd, gkd])[None, :], (128, 384)))
    gq = np.stack([np.asarray(inp[n], f)[0] for n in ("nsa_q_gain", "dsa_q_gain", "nsa_kc_gain")])
    d["gq"] = np.ascontiguousarray(np.broadcast_to(gq[None], (128, 3, 64)))
    w1 = []
    for n in ("cmp_k_w1", "cmp_v_w1"):
        w = np.asarray(inp[n], f)[0].reshape(32, 64, 256).transpose(1, 0, 2)
        w1.append(np.concatenate([w, w], axis=0))
    d["w1"] = np.ascontiguousarray(np.stack(w1))
    b1 = np.stack([np.asarray(inp[n], f)[0].reshape(2, 128) for n in ("cmp_k_b1", "cmp_v_b1")])
    d["b1"] = np.ascontiguousarray(b1.transpose(2, 0, 1).reshape(128, 4))
    w2 = np.stack([np.asarray(inp[n], f)[0].reshape(2, 128, 64) for n in ("cmp_k_w2", "cmp_v_w2")])
    d["w2"] = np.ascontiguousarray(w2.transpose(2, 0, 1, 3))
    pe = np.stack([np.asarray(inp[n], f)[0].T for n in ("cmp_pe_k", "cmp_pe_v")])
    pe = pe.transpose(1, 0, 2)
    d["peT"] = np.ascontiguousarray(np.concatenate([pe, pe], axis=0))
    return d


def make_in_maps(inp):
    x = np.asarray(inp["x"], np.float32)
    wd = _prep_weights(inp)
    maps = []
    for c in range(8):
        b, p = c // 2, c % 2
        cst, rows = _consts(p)
        m = dict(wd)
        m.update(cst)
        m["x_all"] = np.ascontiguousarray(x[b])
        m["x_own"] = np.ascontiguousarray(x[b][rows])
        maps.append((m, b, rows))
    return maps


_NC_CACHE = {}


def kernel(**inputs):
    if "nc" not in _NC_CACHE:
        _NC_CACHE["nc"] = build()
    nc = _NC_CACHE["nc"]
    maps = make_in_maps(inputs)
    res = run_bass_kernel_spmd(nc, [m for m, _, _ in maps], core_ids=list(range(8)))
    out = np.empty((4, S_LEN, 1024), np.float32)
    for c, (_, b, rows) in enumerate(maps):
        out[b][rows] = res.results[c]["y"]
    return out
```

```python
import numpy as np
import ml_dtypes
from contextlib import ExitStack
import concourse.bass as bass
import concourse.mybir as mybir
from concourse.bass_utils import run_bass_kernel_spmd

F32 = mybir.dt.float32
BF16 = mybir.dt.bfloat16
ALU = mybir.AluOpType
AF = mybir.ActivationFunctionType
AX = mybir.AxisListType

ENGS = ['pe', 'act', 'dve', 'pool', 'sp']
NDMA = 12
NSW = 4
S_LEN = 4096
NT = 32
NQ = 16
NEGM = -30000.0
NIT = 15
STEP0 = 64.0
EPS = 1e-6
DEBUG = {}


class Buf:
    __slots__ = ('name', 'w', 'r')

    def __init__(self, name):
        self.name = name
        self.w = None
        self.r = []


class Sched:
    def __init__(self):
        self.ops = {e: [] for e in ENGS}
        self.cnt = {e: 0 for e in ENGS}
        self.seen = {e: {} for e in ENGS}
        self.dma_tgt = [0] * (NDMA + NSW)
        self.dma_next = 0
        self.sw_next = 0
        self.dead = False

    def _wait(self, eng, dep):
        if dep is None:
            return
        key, val = dep
        if key == eng and eng in ('pe', 'sp'):
            return
        if self.seen[eng].get(key, 0) >= val:
            return
        self.seen[eng][key] = val
        self.ops[eng].append(('wait', key, val))

    def op(self, eng, fn, reads=(), writes=(), dma=False):
        if self.dead:
            return None
        for b in reads:
            self._wait(eng, b.w)
        for b in writes:
            self._wait(eng, b.w)
            for d in b.r:
                self._wait(eng, d)
        if eng == 'sp' or dma:
            if dma:
                k = NDMA + self.sw_next
                self.sw_next = (self.sw_next + 1) % NSW
            else:
                k = self.dma_next
                self.dma_next = (k + 1) % NDMA
            key = ('dma', k)
            if self.dma_tgt[k] > 0:
                self._wait(eng, (key, self.dma_tgt[k]))
            self.dma_tgt[k] += 16
            dep = (key, self.dma_tgt[k])
            self.ops[eng].append(('dma', fn, k))
        else:
            self.cnt[eng] += 1
            dep = (eng, self.cnt[eng])
            self.ops[eng].append(('op', fn))
        for b in reads:
            b.r.append(dep)
        for b in writes:
            b.w = dep
            b.r = []
        return dep

    def barrier(self):
        if self.dead:
            return
        deps = [(e, self.cnt[e]) for e in ['pe', 'act', 'dve', 'pool'] if self.cnt[e] > 0]
        deps += [(('dma', k), self.dma_tgt[k]) for k in range(NDMA + NSW) if self.dma_tgt[k] > 0]
        for e in ENGS:
            for d in deps:
                self._wait(e, d)

    def emit(self, block, sems, dma_sems):
        import bisect
        ref = {e: set() for e in ENGS}
        for e in ENGS:
            for item in self.ops[e]:
                if item[0] == 'wait' and not isinstance(item[1], tuple):
                    ref[item[1]].add(item[2])
        refl = {e: sorted(v) for e, v in ref.items()}

        def run(name, eng):
            idx = 0
            for item in self.ops[name]:
                if item[0] == 'wait':
                    key, val = item[1], item[2]
                    if isinstance(key, tuple):
                        eng.wait_ge(dma_sems[key[1]], val)
                    else:
                        eng.wait_ge(sems[key], bisect.bisect_right(refl[key], val))
                elif item[0] == 'op':
                    idx += 1
                    inst = item[1](eng)
                    if idx in ref[name]:
                        inst.then_inc(sems[name], 1)
                else:
                    item[1](eng).then_inc(dma_sems[item[2]], 16)

        @block.tensor
        def _(e):
            run('pe', e)

        @block.scalar
        def _(e):
            run('act', e)

        @block.vector
        def _(e):
            run('dve', e)

        @block.gpsimd
        def _(e):
            run('pool', e)

        @block.sync
        def _(e):
            run('sp', e)
            for k in range(NDMA + NSW):
                if self.dma_tgt[k] > 0:
                    e.wait_ge(dma_sems[k], self.dma_tgt[k])


class _Stop(Exception):
    pass


def build(nq=NQ, nt=NT, dbg=(), stage=99):
    nc = bass.Bass("TRN2", target_bir_lowering=False)
    S = Sched()

    def din(name, shape, dt=F32):
        return nc.dram_tensor(name, list(shape), dt, kind="ExternalInput").ap()

    x_all = din("x_all", [S_LEN, 1024])
    x_own = din("x_own", [NQ * 128, 1024])
    wK_d = din("wK", [128, 8, 1088])
    wQ_d = din("wQ", [128, 8, 2332])
    wO_d = din("wO", [128, 8, 1024])
    ng_d = din("ng", [128, 1024])
    gk_d = din("gk", [128, 384])
    gq_d = din("gq", [128, 3, 64])
    w1_d = din("w1", [2, 128, 32, 256])
    b1_d = din("b1", [128, 4])
    w2_d = din("w2", [128, 2, 2, 64])
    pe_d = din("peT", [128, 2, 32])
    ident_d = din("ident", [128, 128], BF16)
    i4_d = din("i4", [128, 512], BF16)
    mab_d = din("mab", [128, 256], BF16)
    mabf_d = din("mabf", [128, 256])
    win_d = din("winneg", [128, 6, 128], BF16)
    cmpneg_d = din("cmpneg", [NQ * 128, 256], BF16)
    ovl_d = din("ovl", [128, 2, 64], BF16)
    tab_d = din("nsatab", [NQ * 128, 64])
    cs_all_d = din("cs_all", [S_LEN, 128])
    cs_own_d = din("cs_own", [NQ * 128, 128])
    cs_cmp_d = din("cs_cmp", [256, 128])
    y = nc.dram_tensor("y", [NQ * 128, 1024], F32, kind="ExternalOutput").ap()
    dbg_out = {}
    for name, shape in dbg:
        dbg_out[name] = nc.dram_tensor("dbg_" + name, list(shape), F32, kind="ExternalOutput").ap()

    with ExitStack() as es:
        def mark(n):
            if stage == n:
                S.dead = True

        def sb(name, shape, dt=F32):
            return es.enter_context(nc.sbuf_tensor("s_" + name, list(shape), dt))

        def ps(name, shape, dt=F32):
            return es.enter_context(nc.psum_tensor("p_" + name, list(shape), dt))

        def dma(out, in_, reads=(), writes=()):
            S.op('sp', lambda e: e.dma_start(out=out, in_=in_), reads, writes)

        def mm(out, lhsT, rhs, start, stop, reads, writes):
            S.op('pe', lambda e: e.matmul(out, lhsT=lhsT, rhs=rhs, start=start, stop=stop,
                                          skip_group_check=True), reads, writes)

        def tr(out, in_, reads, writes):
            S.op('pe', lambda e: e.transpose(out=out, in_=in_, identity=ident[:]), list(reads) + [b_const], writes)

        def act(out, in_, func, reads, writes, scale=None, bias=None, accum_out=None):
            kw = {}
            if scale is not None:
                kw['scale'] = scale
            if bias is not None:
                kw['bias'] = bias
            if accum_out is not None:
                kw['accum_out'] = accum_out
            S.op('act', lambda e: e.activation(out=out, in_=in_, func=func, **kw), reads, writes)

        def tt(eng, out, in0, in1, op, reads, writes):
            S.op(eng, lambda e: e.tensor_tensor(out=out, in0=in0, in1=in1, op=op), reads, writes)

        def ts(eng, out, in0, s1, s2, op0, op1, reads, writes, accum_out=None):
            if op1 is None:
                S.op(eng, lambda e: e.tensor_scalar(out=out, in0=in0, scalar1=s1, scalar2=None, op0=op0), reads, writes)
            elif accum_out is not None:
                S.op(eng, lambda e: e.tensor_scalar(out=out, in0=in0, scalar1=s1, scalar2=None, op0=op0, op1=op1,
                                                    accum_out=accum_out), reads, writes)
            else:
                S.op(eng, lambda e: e.tensor_scalar(out=out, in0=in0, scalar1=s1, scalar2=s2, op0=op0, op1=op1),
                     reads, writes)

        def stt(out, in0, scalar, in1, op0, op1, reads, writes):
            S.op('dve', lambda e: e.scalar_tensor_tensor(out=out, in0=in0, scalar=scalar, in1=in1, op0=op0, op1=op1),
                 reads, writes)

        def cp(eng, out, in_, reads, writes):
            S.op(eng, lambda e: e.tensor_copy(out=out, in_=in_), reads, writes)

        def red(out, in_, op, reads, writes):
            S.op('dve', lambda e: e.tensor_reduce(out=out, in_=in_, axis=AX.X, op=op), reads, writes)

        def recip(out, in_, reads, writes):
            S.op('dve', lambda e: e.reciprocal(out=out, in_=in_), reads, writes)

        def memset(eng, ap, val, writes):
            S.op(eng, lambda e: e.memset(ap, val), (), writes)

        def dump(name, src_ap, b):
            if name in dbg_out:
                dma(dbg_out[name], src_ap, reads=[b])

        b_const = Buf("const")
        ident = sb("ident", [128, 128], BF16)
        i4 = sb("i4", [128, 512], BF16)
        mab = sb("mab", [128, 256], BF16)
        mabf = sb("mabf", [128, 256])
        winneg = sb("winneg", [128, 6, 128], BF16)
        ngain = sb("ngain", [128, 1024])
        gq = sb("gq", [128, 3, 64])
        KT = sb("KT", [128, 4, S_LEN], BF16)
        b_KT = [Buf("KT%d" % t) for t in range(NT)]
        Vaug = sb("Vaug", [128, NT, 3, 2, 65], BF16)
        b_V = [Buf("V%d" % t) for t in range(NT)]
        KcT = sb("KcT", [128, 256], BF16)
        b_KcT = Buf("KcT")
        VcA = sb("VcA", [128, 2, 2, 128], BF16)
        b_VcA = Buf("VcA")
        epsb = sb("epsb", [128, 1])
        wQb = sb("wQb", [128, 8, 2332], BF16)
        b_wQ = Buf("wQb")
        wOb = sb("wOb", [128, 8, 1024], BF16)
        b_wO = Buf("wOb")

        PB = [ps("pb%d" % k, [128, 512]) for k in range(6)]
        b_PB = [Buf("pb%d" % k) for k in range(6)]
        T0 = ps("t0", [128, 8, 128], BF16)
        b_T0 = Buf("t0")
        T1 = ps("t1", [128, 8, 128], BF16)
        b_T1 = Buf("t1")

        SCR = {}

        def alloc_scratch(sbf, nb, tag):
            SCR['n'] = nb
            SCR['wstw'] = 1200 if nb == 2 else 600
            for nm, shape, dt in [("xt", [128, 1024], F32), ("xn", [128, 1024], BF16), ("hT", [128, 8, 128], BF16),
                                  ("st1", [128, 4], F32), ("cs", [128, 128], F32), ("kbuf", [128, 8, 64], F32),
                                  ("ktmp", [128, 8, 64], F32), ("ktmp2", [128, 8, 64], F32), ("kst", [128, 8], F32),
                                  ("kst2", [128, 8], F32), ("krot", [128, 8, 64], BF16)]:
                nbb = 1 if nm in ("ktmp", "ktmp2") else nb
                SCR[nm] = [sbf("%s%s%d" % (nm, tag, k), shape, dt) for k in range(nbb)]
                SCR["b_" + nm] = [Buf("%s%s%d" % (nm, tag, k)) for k in range(nbb)]
        wst_n = [0]

        dma(ident[:], ident_d[:, :], writes=[b_const])
        dma(i4[:], i4_d[:, :], writes=[b_const])
        dma(mab[:], mab_d[:, :], writes=[b_const])
        dma(mabf[:], mabf_d[:, :], writes=[b_const])
        dma(winneg[:], win_d[:, :, :], writes=[b_const])
        dma(ngain[:], ng_d[:, :], writes=[b_const])
        dma(gq[:], gq_d[:, :, :], writes=[b_const])
        memset('pool', epsb[:], EPS, [b_const])
        memset('pool', Vaug[:, :, :, :, 64:65], 1.0, b_V)

        WT = {'tasks': [], 'n': 0}

        def load_weight_bf16(dst_fn, src_fn, ncols_total, b_dst, nrows=8):
            nch = (ncols_total + 599) // 600
            cw = (ncols_total + nch - 1) // nch
            for r in range(nrows):
                for c in range(nch):
                    c0, c1 = c * cw, min((c + 1) * cw, ncols_total)
                    WT['tasks'].append((dst_fn(r, c0, c1), src_fn(r, c0, c1), c1 - c0, b_dst))

        def pump(n):
            for _ in range(n):
                if not WT['tasks']:
                    return
                dst, src, w, b_dst = WT['tasks'].pop(0)
                k = WT['n'] % 2
                WT['n'] += 1
                dma(WT['wst'][k][:, 0:w], src, writes=[WT['b_wst'][k]])
                cp('dve', dst, WT['wst'][k][:, 0:w], [WT['b_wst'][k]], [b_dst])

        WT['pend'] = []

        def pump_cast():
            for (dst, k, w, b_dst) in WT['pend']:
                cp('dve', dst, WT['wst'][k][:, 0:w], [WT['b_wst'][k]], [b_dst])
            WT['pend'] = []

        def pump_dma(n):
            for _ in range(n):
                if not WT['tasks'] or len(WT['pend']) >= 2:
                    return
                dst, src, w, b_dst = WT['tasks'].pop(0)
                k = WT['n'] % 2
                WT['n'] += 1
                dma(WT['wst'][k][:, 0:w], src, writes=[WT['b_wst'][k]])
                WT['pend'].append((dst, k, w, b_dst))

        def rms_to_hT(src_dram_tile, k):
            dma(xt[k][:], src_dram_tile, writes=[b_xt[k]])
            act(xn[k][:], xt[k][:], AF.Square, [b_xt[k]], [b_xn[k], b_st[k]], accum_out=st1[k][:, 0:1])
            act(st1[k][:, 1:2], st1[k][:, 0:1], AF.Sqrt, [b_st[k], b_const], [b_st[k]], scale=1.0 / 1024.0, bias=epsb[:, 0:1])
            recip(st1[k][:, 2:3], st1[k][:, 1:2], [b_st[k]], [b_st[k]])
            stt(xn[k][:], xt[k][:], st1[k][:, 2:3], ngain[:], ALU.mult, ALU.mult, [b_xt[k], b_st[k], b_const], [b_xn[k]])
            for c in range(8):
                tr(T0[:, c, :], xn[k][:, c * 128:(c + 1) * 128], [b_xn[k]], [b_T0])
            act(hT[k][:], T0[:], AF.Copy, [b_T0], [b_hT[k]])

        def norm_rope(k, nh_norm, nh, gain_ap, out_ap, b_out, extra_reads=(), kb=None, b_kb=None, pre_sd=None, b_pre=None):
            if kb is None:
                kb = kbuf[k]
                b_kb = b_kbuf[k]
            if nh_norm > 0 and pre_sd is not None:
                tt('dve', kb[:, 0:nh_norm, :], kb[:, 0:nh_norm, :],
                   pre_sd.unsqueeze(2).to_broadcast([128, nh_norm, 64]), ALU.mult,
                   [b_kb, b_pre], [b_kb])
                tt('dve', kb[:, 0:nh_norm, :], kb[:, 0:nh_norm, :], gain_ap, ALU.mult,
                   [b_kb, b_const], [b_kb])
            elif nh_norm > 0:
                tt('dve', ktmp[0][:, 0:nh_norm, :], kb[:, 0:nh_norm, :], kb[:, 0:nh_norm, :], ALU.mult,
                   [b_kb], [b_ktmp[0]])
                red(kst[k][:, 0:nh_norm], ktmp[0][:, 0:nh_norm, :], ALU.add, [b_ktmp[0]], [b_kst[k]])
                act(kst2[k][:, 0:nh_norm], kst[k][:, 0:nh_norm], AF.Sqrt, [b_kst[k], b_const], [b_kst[k]],
                    scale=1.0 / 64.0, bias=epsb[:, 0:1])
                recip(kst[k][:, 0:nh_norm], kst2[k][:, 0:nh_norm], [b_kst[k]], [b_kst[k]])
                tt('dve', kb[:, 0:nh_norm, :], kb[:, 0:nh_norm, :],
                   kst[k][:, 0:nh_norm].unsqueeze(2).to_broadcast([128, nh_norm, 64]), ALU.mult,
                   [b_kb, b_kst[k]], [b_kb])
                tt('dve', kb[:, 0:nh_norm, :], kb[:, 0:nh_norm, :], gain_ap, ALU.mult,
                   [b_kb, b_const], [b_kb])
            cC = cs[k][:, 0:64].unsqueeze(1).to_broadcast([128, nh, 64])
            sA = cs[k][:, 64:96].unsqueeze(1).to_broadcast([128, nh, 32])
            sB = cs[k][:, 96:128].unsqueeze(1).to_broadcast([128, nh, 32])
            tt('dve', ktmp[0][:, 0:nh, :], kb[:, 0:nh, :], cC, ALU.mult, [b_kb, b_cs[k]], [b_ktmp[0]])
            tt('pool', ktmp2[0][:, 0:nh, 0:32], kb[:, 0:nh, 32:64], sA, ALU.mult, [b_kb, b_cs[k]], [b_ktmp2[0]])
            tt('pool', ktmp2[0][:, 0:nh, 32:64], kb[:, 0:nh, 0:32], sB, ALU.mult, [b_kb, b_cs[k]], [b_ktmp2[0]])
            tt('dve', out_ap, ktmp[0][:, 0:nh, :], ktmp2[0][:, 0:nh, :], ALU.add,
               [b_ktmp[0], b_ktmp2[0]] + list(extra_reads), [b_out])

        try:
            with ExitStack() as es1:
                def sb1(name, shape, dt=F32):
                    return es1.enter_context(nc.sbuf_tensor("s_" + name, list(shape), dt))
                alloc_scratch(sb1, 2, "a")
                xt, xn, hT, st1, cs, kbuf, ktmp, ktmp2, kst, kst2, krot = (SCR[n_] for n_ in
                    ["xt", "xn", "hT", "st1", "cs", "kbuf", "ktmp", "ktmp2", "kst", "kst2", "krot"])
                b_xt, b_xn, b_hT, b_st, b_cs, b_kbuf, b_ktmp, b_ktmp2, b_kst, b_krot = (SCR["b_" + n_] for n_ in
                    ["xt", "xn", "hT", "st1", "cs", "kbuf", "ktmp", "ktmp2", "kst", "krot"])
                WT['wst'] = [sb1("wst%d" % k, [128, 600]) for k in range(2)]
                WT['b_wst'] = [Buf("wst%d" % k) for k in range(2)]
                gk = sb1("gk", [128, 6, 64])
                dma(gk[:], gk_d.rearrange("p (a b) -> p a b", b=64), writes=[b_const])
                wKb = sb1("wKb", [128, 8, 1088], BF16)
                b_wK = Buf("wKb")
                rawT = sb1("rawT", [128, 2, S_LEN], BF16)
                b_raw = Buf("rawT")
                w1b1 = sb1("w1b", [128, 32, 256], BF16)
                b_w11 = Buf("w1b")
                kss = [sb1("kss%d" % k, [128, 16]) for k in range(2)]
                b_kss = [Buf("kss%d" % k) for k in range(2)]
                sqj = [sb1("sqj%d" % k, [128, 64], BF16) for k in range(2)]
                kvraw = [sb1("kvraw%d" % k, [128, 256], BF16) for k in range(2)]
                b_kvraw = [Buf("kvraw%d" % k) for k in range(2)]
                hidT = sb1("hidT", [128, 2, 256], BF16)
                b_hid = Buf("hidT")
                cbias = sb1("cbias", [128, 4])
                b_cb = Buf("cbias")
                b1s = sb1("b1s", [128, 4])
                w2f = sb1("w2f", [128, 2, 2, 64])
                w2b = sb1("w2b", [128, 2, 2, 64], BF16)
                pef = sb1("pef", [128, 2, 32])
                peb = sb1("peb", [128, 2, 32], BF16)
                ovl = sb1("ovl", [128, 2, 64], BF16)
                b_c1 = Buf("c1")

                w1v = wKb[:].rearrange("p a b -> p (a b)")[:, 0:8192].rearrange("p (l j) -> p l j", j=256)
                w1b = [w1b1, w1v]
                b_w1 = [b_w11, b_wK]
                stgA = w1b1[:].rearrange("p a b -> p (a b)").bitcast(F32)
                stgB = rawT[:].rearrange("p a b -> p (a b)").bitcast(F32)
                for stg, b_stg, r0 in ((stgA, b_w11, 0), (stgB, b_raw, 3)):
                    dma(stg[:, 0:3264].rearrange("p (a b) -> p a b", b=1088), wK_d[:, r0:r0 + 3, :], writes=[b_stg])
                load_weight_bf16(lambda r, c0, c1: wKb[:, 6 + r, c0:c1], lambda r, c0, c1: wK_d[:, 6 + r, c0:c1], 1088, b_wK, nrows=2)
                pump(1000)
                for stg, b_stg, r0 in ((stgA, b_w11, 0), (stgB, b_raw, 3)):
                    for r in range(3):
                        cp('dve' if r != 1 else 'pool', wKb[:, r0 + r, :], stg[:, r * 1088:(r + 1) * 1088], [b_stg], [b_wK])

                def load_w1(kv):
                    load_weight_bf16(lambda r, c0, c1: w1b[kv][:, r * 4:(r + 1) * 4, :].rearrange("p a b -> p (a b)")[:, c0:c1],
                                     lambda r, c0, c1: w1_d[kv, :, r * 4:(r + 1) * 4, :].rearrange("p a b -> p (a b)")[:, c0:c1],
                                     1024, b_w1[kv], nrows=8)
                dma(b1s[:], b1_d[:, :], writes=[b_c1])
                dma(w2f[:], w2_d[:, :, :, :], writes=[b_c1])
                dma(pef[:], pe_d[:, :, :], writes=[b_c1])
                dma(ovl[:], ovl_d[:, :, :], writes=[b_c1])
                cp('pool', w2b[:], w2f[:], [b_c1], [b_c1])
                cp('pool', peb[:], pef[:], [b_c1], [b_c1])

                mark(1)
                b_T1a = b_T1

                def F1(t):
                    k = t % 2
                    dma(xt[k][:], x_all[t * 128:(t + 1) * 128, :], writes=[b_xt[k]])
                    act(xn[k][:], xt[k][:], AF.Square, [b_xt[k]], [b_xn[k], b_st[k]], accum_out=st1[k][:, 0:1])
                    act(st1[k][:, 1:2], st1[k][:, 0:1], AF.Sqrt, [b_st[k], b_const], [b_st[k]], scale=1.0 / 1024.0, bias=epsb[:, 0:1])
                    recip(st1[k][:, 2:3], st1[k][:, 1:2], [b_st[k]], [b_st[k]])
                    stt(xn[k][:], xt[k][:], st1[k][:, 2:3], ngain[:], ALU.mult, ALU.mult, [b_xt[k], b_st[k], b_const], [b_xn[k]])

                def F2(t):
                    k = t % 2
                    for c in range(8):
                        tr(T0[:, c, :], xn[k][:, c * 128:(c + 1) * 128], [b_xn[k]], [b_T0])
                    act(hT[k][:], T0[:], AF.Copy, [b_T0], [b_hT[k]])

                def M(t):
                    k = t % 2
                    bA, bB, bC = (0, 1, 2) if k == 0 else (3, 4, 5)
                    for kc in range(8):
                        mm(PB[bA][:, 0:448], hT[k][:, kc, :], wKb[:, kc, 0:448], kc == 0, kc == 7, [b_hT[k], b_wK], [b_PB[bA]])
                    for kc in range(8):
                        mm(PB[bB][:, 0:256], hT[k][:, kc, :], wKb[:, kc, 448:704], kc == 0, kc == 7, [b_hT[k], b_wK], [b_PB[bB]])
                    for kc in range(8):
                        mm(PB[bC][:, 0:384], hT[k][:, kc, :], wKb[:, kc, 704:1088], kc == 0, kc == 7, [b_hT[k], b_wK], [b_PB[bC]])
                    act(kvraw[k][:], PB[bB][:, 0:256], AF.Copy, [b_PB[bB]], [b_kvraw[k]])
                    act(kbuf[k][:, 0:7, :], PB[bA][:, 0:448].rearrange("p (a b) -> p a b", b=64), AF.Copy, [b_PB[bA]], [b_kbuf[k]])
                    act(kbuf[k][:, 7, :], PB[bA][:, 384:448], AF.Copy, [b_PB[bA]], [b_kbuf[k]])
                    act(Vaug[:, t, :, :, 0:64], PB[bC][:, 0:384].rearrange("p (a b c) -> p a b c", a=3, b=2), AF.Copy,
                        [b_PB[bC]], [b_V[t]])
                    for j in range(2):
                        tr(T1[:, 4 + j, :], kvraw[k][:, j * 128:(j + 1) * 128], [b_kvraw[k]], [b_T1a])
                    act(rawT[:].rearrange("p k (j c) -> p k c j", j=16)[:, :, t * 8:(t + 1) * 8, :],
                        T1[:, 4:6, :].rearrange("p k (c j) -> p k c j", j=16), AF.Copy, [b_T1a], [b_raw])

                def B1(t):
                    k = t % 2
                    norm_rope(k, 6, 8, gk[:, :, :], krot[k][:], b_krot[k])

                def B2(t):
                    k = t % 2
                    for j in range(4):
                        tr(T1[:, j, :], krot[k][:, 2 * j:2 * j + 2, :].rearrange("p a b -> p (a b)"), [b_krot[k]], [b_T1])
                    act(KT[:, :, t * 128:(t + 1) * 128], T1[:, 0:4, :], AF.Copy, [b_T1], [b_KT[t]])

                def load_cs(t):
                    dma(cs[t % 2][:], cs_all_d[t * 128:(t + 1) * 128, :], writes=[b_cs[t % 2]])

                F1(0)
                load_cs(0)
                F2(0)
                if nt > 1:
                    F1(1)
                for t in range(nt):
                    if t + 2 < nt:
                        F1(t + 2)
                    pump_cast()
                    if t + 1 < nt:
                        F2(t + 1)
                    M(t)
                    B1(t)
                    if t + 1 < nt:
                        load_cs(t + 1)
                    if t >= 1:
                        B2(t - 1)
                    if t == 0:
                        load_w1(0)
                        load_weight_bf16(lambda r, c0, c1: wQb[:, r, c0:c1], lambda r, c0, c1: wQ_d[:, r, c0:c1], 2332, b_wQ)
                        load_weight_bf16(lambda r, c0, c1: wOb[:, r, c0:c1], lambda r, c0, c1: wO_d[:, r, c0:c1], 1024, b_wO)
                    pump_dma(2)
                B2(nt - 1)
                pump_cast()
                pump(1000)
                mark(3)
                memset('pool', hidT[:], 0.0, [b_hid])
                if nt < NT:
                    memset('pool', rawT[:].rearrange("p k (j c) -> p k j c", j=16)[:, :, :, nt * 8:], 0.0, [b_raw])
                for g in range(2):
                    cp('pool', VcA[:, :, g, 64:128], ovl[:], [b_c1], [b_VcA])
                ck = 0
                dma(cs[ck][:], cs_cmp_d[0:128, :], writes=[b_cs[ck]])
                dma(cs[1][:], cs_cmp_d[128:256, :], writes=[b_cs[1]])
                for kv in range(2):
                    for hh in range(2):
                        bk = 0
                        for l in range(32):
                            mm(PB[bk][:, 0:1], w1b[kv][0:64, l, hh * 128:(hh + 1) * 128], peb[0:64, kv, l:l + 1],
                               l == 0, l == 31, [b_w1[kv], b_c1], [b_PB[bk]])
                        tt('dve', cbias[:, kv * 2 + hh:kv * 2 + hh + 1], PB[bk][:, 0:1], b1s[:, kv * 2 + hh:kv * 2 + hh + 1],
                           ALU.add, [b_PB[bk], b_c1], [b_cb])
                    if kv == 0:
                        load_w1(1)
                        pump(1000)
                    for g in range(2):
                        gs = slice(g * 64, (g + 1) * 64)
                        for hh in range(2):
                            bk = 1 + hh
                            for l in range(32):
                                mm(PB[bk][:, 0:255], w1b[kv][gs, l, hh * 128:(hh + 1) * 128],
                                   rawT[gs, kv, (l % 16) * 256 + l // 16:(l % 16) * 256 + l // 16 + 255], l == 0, l == 31,
                                   [b_w1[kv], b_raw], [b_PB[bk]])
                            act(hidT[:, hh, 0:255], PB[bk][:, 0:255], AF.Silu, [b_PB[bk], b_cb], [b_hid],
                                bias=cbias[:, kv * 2 + hh:kv * 2 + hh + 1])
                        for ct in range(2):
                            bk = 3 + ct
                            for hh in range(2):
                                mm(PB[bk][:, 0:64], hidT[:, hh, ct * 128:(ct + 1) * 128], w2b[:, kv, hh, :],
                                   hh == 0, hh == 1, [b_hid, b_c1], [b_PB[bk]])
                            if kv == 0:
                                act(kbuf[ct][:, g, :], PB[bk][:, 0:64], AF.Copy, [b_PB[bk]], [b_kbuf[ct]])
                            else:
                                act(VcA[:, ct, g, 0:64], PB[bk][:, 0:64], AF.Copy, [b_PB[bk]], [b_VcA])
                    if kv == 0:
                        for ct in range(2):
                            norm_rope(ct, 2, 2, gq[:, 2:3, :].to_broadcast([128, 2, 64]), krot[ct][:, 0:2, :], b_krot[ct])
                            tr(T1[:, ct, :], krot[ct][:, 0:2, :].rearrange("p a b -> p (a b)"), [b_krot[ct]], [b_T1])
                        act(KcT[:].rearrange("p (a b) -> p a b", b=128), T1[:, 0:2, :], AF.Copy, [b_T1], [b_KcT])
                mark(4)
                S.barrier()

            if "KT" in dbg_out:
                for j in range(4):
                    for t in range(nt):
                        pass
            dbgst = sb("dbgst", [128, 1024]) if dbg_out else None
            b_dbgst = Buf("dbgst")

            def dump_bf(name, src_ap, n, b):
                if name in dbg_out:
                    cp('dve', dbgst[:, 0:n], src_ap, [b], [b_dbgst])
                    dma(dbg_out[name], dbgst[:, 0:n], reads=[b_dbgst])

            for j in range(4):
                for c in range(4):
                    if "KT%d_%d" % (j, c) in dbg_out:
                        cp('dve', dbgst[:, :], KT[:, j, c * 1024:(c + 1) * 1024], b_KT, [b_dbgst])
                        dma(dbg_out["KT%d_%d" % (j, c)], dbgst[:, :], reads=[b_dbgst])
            dump_bf("KcT", KcT[:], 256, b_KcT)
            dump_bf("VcA", VcA[:].rearrange("p a b c -> p (a b c)"), 512, b_VcA)
            dump_bf("V0", Vaug[:, 0, :, :, :].rearrange("p a b c -> p (a b c)"), 390, b_V[0])

            with ExitStack() as es2:
                def sb2(name, shape, dt=F32):
                    return es2.enter_context(nc.sbuf_tensor("s_" + name, list(shape), dt))
                SC = sb2("SC", [128, S_LEN])
                b_SC = Buf("SC")
                SCR['wstw'] = 600
                for nm, shape, dt in [("xn", [128, 1024], BF16), ("hT", [128, 8, 128], BF16),
                                      ("st1", [128, 4], F32), ("cs", [128, 128], F32),
                                      ("ktmp", [128, 8, 64], F32), ("ktmp2", [128, 8, 64], F32), ("kst", [128, 8], F32),
                                      ("kst2", [128, 8], F32)]:
                    SCR[nm] = [sb2("%sb" % nm, shape, dt)]
                    SCR["b_" + nm] = [Buf("%sb" % nm)]
                SCR["xt"] = [sb2("xtb%d" % k, [128, 1024]) for k in range(2)]
                SCR["b_xt"] = [Buf("xtb%d" % k) for k in range(2)]
                kq = [sb2("kq%d" % j, [128, 8, 64]) for j in range(2)] + [sb2("kq2", [128, 4, 64])]
                b_kq = [Buf("kq%d" % j) for j in range(3)]
                xt, xn, hT, st1, cs, ktmp, ktmp2, kst, kst2 = (SCR[n_] for n_ in
                    ["xt", "xn", "hT", "st1", "cs", "ktmp", "ktmp2", "kst", "kst2"])
                b_xt, b_xn, b_hT, b_st, b_cs, b_ktmp, b_ktmp2, b_kst = (SCR["b_" + n_] for n_ in
                    ["xt", "xn", "hT", "st1", "cs", "ktmp", "ktmp2", "kst"])
                QnT = [[sb2("QnT%d_%d" % (k, g), [128, 4, 128], BF16) for g in range(2)] for k in range(2)]
                QdT = [[sb2("QdT%d_%d" % (k, g), [128, 4, 128], BF16) for g in range(2)] for k in range(2)]
                b_QnT = [Buf("QnT%d" % k) for k in range(2)]
                b_QdT = [Buf("QdT%d" % k) for k in range(2)]
                QiT = sb2("QiT", [128, 4, 128], BF16)
                b_QiT = Buf("QiT")
                for k in range(2):
                    for g in range(2):
                        memset('pool', QnT[k][g][:], 0.0, [b_QnT[k]])
                        memset('pool', QdT[k][g][:], 0.0, [b_QdT[k]])
                memset('pool', QiT[:], 0.0, [b_QiT])
                Zs = [sb2("Zs%d" % k, [128, 1024], BF16) for k in range(2)]
                b_Zs = [Buf("Zs%d" % k) for k in range(2)]
                G = [sb2("G%d" % k, [128, 24]) for k in range(2)]
                b_G = [Buf("G%d" % k) for k in range(2)]
                wi = sb2("wi", [128, 12])
                b_wi = Buf("wi")
                qst = sb2("qst", [128, 2, 16])
                b_qst = [Buf("qst0"), Buf("qst1")]
                qrot = sb2("qrot", [128, 4, 2, 64], BF16)
                b_qrot = Buf("qrot")
                qif = kq[2]
                b_qif = b_kq[2]
                negd = sb2("negd", [128, S_LEN], BF16)
                b_negd = Buf("negd")
                junk8 = sb2("junk8", [128, S_LEN], mybir.dt.uint8)
                b_junk8 = Buf("junk8")
                negs = sb2("negs", [128, S_LEN], BF16)
                b_negs = Buf("negs")
                NET = 2
                ET = [sb2("ET%d" % k, [128, 512], BF16) for k in range(NET)]
                b_ET = [Buf("ET%d" % k) for k in range(NET)]
                R = [ET[0], ET[1], ET[0], ET[1]]
                b_R = [b_ET[0], b_ET[1], b_ET[0], b_ET[1]]
                et_n = [0]
                cmpneg = [sb2("cmpneg%d" % k, [128, 256], BF16) for k in range(2)]
                b_cmpneg = [Buf("cmpneg%d" % k) for k in range(2)]
                tab = [sb2("tab%d" % k, [128, 64]) for k in range(2)]
                b_tab = [Buf("tab%d" % k) for k in range(2)]
                OcmpA = sb2("OcmpA", [128, 8, 128])
                OselA = sb2("OselA", [128, 8, 65])
                OwinA = sb2("OwinA", [128, 8, 65])
                OdsaA = sb2("OdsaA", [128, 8, 65])
                Ocmp = [OcmpA[:, g * 4:(g + 1) * 4, :] for g in range(2)]
                Osel = [OselA[:, g * 4:(g + 1) * 4, :] for g in range(2)]
                Owin = [OwinA[:, g * 4:(g + 1) * 4, :] for g in range(2)]
                Odsa = [OdsaA[:, g * 4:(g + 1) * 4, :] for g in range(2)]
                b_Ocmp = [Buf("Ocmp%d" % g) for g in range(2)]
                b_Osel = [Buf("Osel%d" % g) for g in range(2)]
                b_Owin = [Buf("Owin%d" % g) for g in range(2)]
                b_Odsa = [Buf("Odsa%d" % g) for g in range(2)]
                sm = sb2("sm", [128, 16])
                b_sm = Buf("sm")
                imp = sb2("imp", [128, 64])
                b_imp = Buf("imp")
                m8 = sb2("m8", [128, 16])
                b_m8 = Buf("m8")
                thr = sb2("thr", [128, 4])
                b_thr = Buf("thr")
                tmpA = sb2("tmpA", [128, 4, 64])
                tmpB = sb2("tmpB", [128, 4, 64])
                b_tmpA, b_tmpB = Buf("tmpA"), Buf("tmpB")
                impn, b_impn = tmpA, b_tmpA
                imp2 = tmpB[:, 0, :]
                nsel = [tmpB[:, 1 + g, :] for g in range(2)]
                b_nsel = [b_tmpB, b_tmpB]
                coef = sb2("coef", [128, 32])
                b_coef = Buf("coef")
                mixed = kq[0][:].rearrange("p a b -> p (a b)").bitcast(BF16)
                b_mixed = b_kq[0]
                mixT = hT[0]
                b_mixT = b_hT[0]


                o_n = [0]

                def attention(g, kts, kt_mask_fn, Kj, Vj, QT, b_QT, Odst, b_Odst):
                    ob = 2 + (o_n[0] % 2)
                    o_n[0] += 1
                    qflat = QT[g][:, :, :].rearrange("p a b -> p (a b)")
                    nk = len(kts)
                    es_ = [None] * nk
                    for n in range(nk + 1):
                        if n < nk:
                            kt = kts[n]
                            sbk = n % 2
                            masks = kt_mask_fn(kt)
                            mm(PB[sbk][:, :], KT[:, Kj, kt * 128:(kt + 1) * 128], qflat, True, len(masks) == 0,
                               [b_KT[kt], b_QT], [b_PB[sbk]])
                            for mi, (mlhs, mb) in enumerate(masks):
                                mm(PB[sbk][:, :], mlhs, i4[:], False, mi == len(masks) - 1, [b_const] + list(mb), [b_PB[sbk]])
                            e = et_n[0] % NET
                            et_n[0] += 1
                            es_[n] = e
                            act(ET[e][:], PB[sbk][:, :], AF.Exp, [b_PB[sbk]], [b_ET[e]], scale=0.125)
                        if n >= 1:
                            m = n - 1
                            kt = kts[m]
                            e = es_[m]
                            for r in range(4):
                                mm(PB[ob][:, r * 65:(r + 1) * 65], ET[e][:, r * 128:(r + 1) * 128], Vj(kt),
                                   m == 0 and r == 0, m == nk - 1 and r == 3, [b_ET[e], b_V[kt]], [b_PB[ob]])
                    act(Odst, PB[ob][:, 0:260].rearrange("p (a b) -> p a b", b=65), AF.Copy, [b_PB[ob]], [b_Odst])

                def prep_x_front(i):
                    pp = i % 2
                    px = 0
                    k = 0
                    dma(xt[px][:], x_own[i * 128:(i + 1) * 128, :], writes=[b_xt[px]])
                    dma(cs[k][:], cs_own_d[i * 128:(i + 1) * 128, :], writes=[b_cs[k]])
                    dma(cmpneg[pp][:], cmpneg_d[i * 128:(i + 1) * 128, :], writes=[b_cmpneg[pp]])
                    dma(tab[pp][:], tab_d[i * 128:(i + 1) * 128, :], writes=[b_tab[pp]])
                    act(xn[k][:], xt[px][:], AF.Square, [b_xt[px]], [b_xn[k], b_st[k]], accum_out=st1[k][:, 0:1])
                    act(st1[k][:, 1:2], st1[k][:, 0:1], AF.Ln, [b_st[k], b_const], [b_st[k]], scale=1.0 / 1024.0, bias=epsb[:, 0:1])
                    act(st1[k][:, 2:3], st1[k][:, 1:2], AF.Exp, [b_st[k]], [b_st[k]], scale=-0.5)
                    stt(xn[k][:], xt[px][:], st1[k][:, 2:3], ngain[:], ALU.mult, ALU.mult, [b_xt[px], b_st[k], b_const], [b_xn[k]])

                def prep_x_back(i):
                    k = 0
                    for c in range(8):
                        tr(T0[:, c, :], xn[k][:, c * 128:(c + 1) * 128], [b_xn[k]], [b_T0])
                    act(hT[k][:], T0[:], AF.Copy, [b_T0], [b_hT[k]])

                def qproj(bk, c0, n):
                    for kc in range(8):
                        mm(PB[bk][:, 0:n], hT[0][:, kc, :], wQb[:, kc, c0:c0 + n], kc == 0, kc == 7, [b_hT[0], b_wQ], [b_PB[bk]])

                def prep_qi(i):
                    pp = i % 2
                    qproj(4, 2048, 284)
                    act(kq[2][:].rearrange("p a b -> p (a b)"), PB[4][:, 0:256], AF.Copy, [b_PB[4]], [b_kq[2]])
                    act(G[pp][:], PB[4][:, 256:280], AF.Exp, [b_PB[4]], [b_G[pp]], scale=-1.0)
                    ts('dve', G[pp][:], G[pp][:], 1.0, None, ALU.add, None, [b_G[pp]], [b_G[pp]])
                    recip(G[pp][:], G[pp][:], [b_G[pp]], [b_G[pp]])
                    act(wi[:, 0:4], PB[4][:, 280:284], AF.Copy, [b_PB[4]], [b_wi])

                def prep_qproj(i):
                    pp = i % 2
                    for which, (bk, c0) in enumerate([(5, 0), (4, 1024)]):
                        qproj(bk, c0, 512)
                        act(kq[which][:].rearrange("p a b -> p (a b)"), PB[bk][:, :], AF.Copy, [b_PB[bk]], [b_kq[which]])
                        for h in range(8):
                            act(ET[0][:, 0:64], PB[bk][:, h * 64:(h + 1) * 64], AF.Square, [b_PB[bk]], [b_ET[0], b_qst[which]],
                                accum_out=qst[:, which, h:h + 1])
                        act(qst[:, which, 0:8], qst[:, which, 0:8], AF.Ln, [b_qst[which], b_const], [b_qst[which]],
                            scale=1.0 / 64.0, bias=epsb[:, 0:1])
                        act(qst[:, which, 8:16], qst[:, which, 0:8], AF.Exp, [b_qst[which]], [b_qst[which]], scale=-0.5)
                    qproj(5, 512, 512)
                    act(Zs[pp][:, 0:512], PB[5][:, :], AF.Silu, [b_PB[5]], [b_Zs[pp]])
                    qproj(4, 1536, 512)
                    act(Zs[pp][:, 512:1024], PB[4][:, :], AF.Silu, [b_PB[4]], [b_Zs[pp]])

                def qrot_view(which):
                    return xn[0][:, which * 512:(which + 1) * 512].rearrange("p (r g d) -> p r g d", r=4, g=2)

                def prep_qnorm_dve(i):
                    for which in range(2):
                        norm_rope(0, 8, 8, gq[:, which:which + 1, :].to_broadcast([128, 8, 64]),
                                  qrot_view(which).rearrange("p r g d -> p g r d"), b_xn[0], kb=kq[which], b_kb=b_kq[which],
                                  pre_sd=qst[:, which, 8:16], b_pre=b_qst[which])

                def prep_qnorm_pe(i):
                    pp = i % 2
                    for which, (QTt, b_QTt) in enumerate([(QnT[pp], b_QnT[pp]), (QdT[pp], b_QdT[pp])]):
                        qv = qrot_view(which)
                        for r in range(4):
                            tr(T1[:, r, :], qv[:, r, :, :].rearrange("p a b -> p (a b)"), [b_xn[0]], [b_T1])
                        act(QTt[0][0:64, :, :], T1[0:64, 0:4, :], AF.Copy, [b_T1], [b_QTt])
                        act(QTt[1][64:128, :, :], T1[64:128, 0:4, :], AF.Copy, [b_T1], [b_QTt])

                def prep_idx(i):
                    nkeys = (2 * i + 2) * 128
                    k = 0
                    ts('dve', wi[:, 8:12], wi[:, 0:4], 0.0, 2.0, ALU.is_ge, ALU.mult, [b_wi], [b_wi])
                    ts('dve', wi[:, 8:12], wi[:, 8:12], -1.0, None, ALU.add, None, [b_wi], [b_wi])
                    tt('dve', wi[:, 4:8], wi[:, 0:4], wi[:, 8:12], ALU.mult, [b_wi], [b_wi])
                    norm_rope(k, 0, 4, None, kq[2][:], b_kq[2], kb=kq[2], b_kb=b_kq[2])
                    tt('dve', qrot[:].rearrange("p r g d -> p (r g) d")[:, 0:4, :], kq[2][:],
                       wi[:, 4:8].unsqueeze(2).to_broadcast([128, 4, 64]), ALU.mult, [b_kq[2], b_wi], [b_qrot])
                    for j in range(2):
                        tr(T1[:, j, :], qrot[:].rearrange("p r g d -> p (r g d)")[:, j * 128:(j + 1) * 128], [b_qrot], [b_T1])
                    act(QiT[0:64, 0:4:2, :], T1[0:64, 0:2, :], AF.Copy, [b_T1], [b_QiT])
                    act(QiT[64:128, 1:4:2, :], T1[64:128, 0:2, :], AF.Copy, [b_T1], [b_QiT])
                    nch = (nkeys + 511) // 512
                    for ch in range(nch):
                        w = min(512, nkeys - ch * 512)
                        kts_ch = list(range(ch * 4, ch * 4 + w // 128))
                        for h in range(4):
                            bk = 4 + (h % 2)
                            mm(PB[bk][:, 0:w], QiT[:, h, :], KT[:, 3, ch * 512:ch * 512 + w], True, True,
                               [b_QiT] + [b_KT[t] for t in kts_ch], [b_PB[bk]])
                            act(R[h][:, 0:w], PB[bk][:, 0:w], AF.Relu, [b_PB[bk]], [b_R[h]])
                            scs = SC[:, ch * 512:ch * 512 + w]
                            if h == 0:
                                ts('dve', scs, R[0][:, 0:w], wi[:, 8:9], None, ALU.mult, None, [b_R[0], b_wi], [b_SC])
                            else:
                                stt(scs, R[h][:, 0:w], wi[:, 8 + h:9 + h], scs, ALU.mult, ALU.add, [b_R[h], b_wi, b_SC], [b_SC])
                    tt('dve', SC[:, nkeys - 256:nkeys], SC[:, nkeys - 256:nkeys], mabf[:], ALU.add, [b_SC, b_const], [b_SC])

                def search(i):
                    nkeys = (2 * i + 2) * 128
                    if nkeys <= 256:
                        memset('dve', thr[:, 3:4], -1e29, [b_thr])
                        return
                    memset('dve', thr[:, 0:1], 0.0, [b_thr])
                    for it in range(NIT):
                        sn = STEP0 / (2.0 ** it)
                        ts('dve', junk8[:, 0:nkeys], SC[:, 0:nkeys], thr[:, 0:1], 0.0, ALU.is_ge, ALU.add,
                           [b_SC, b_thr], [b_junk8, b_thr], accum_out=thr[:, 1:2])
                        ts('dve', thr[:, 2:3], thr[:, 1:2], 255.5, 2.0 * sn, ALU.is_ge, ALU.mult, [b_thr], [b_thr])
                        stt(thr[:, 0:1], thr[:, 2:3], -sn, thr[:, 0:1], ALU.add, ALU.add, [b_thr], [b_thr])
                    ts('dve', thr[:, 3:4], thr[:, 0:1], -STEP0 / (2.0 ** (NIT - 1)), None, ALU.add, None, [b_thr], [b_thr])

                def search_final(i):
                    nkeys = (2 * i + 2) * 128
                    ts('dve', negd[:, 0:nkeys], SC[:, 0:nkeys], thr[:, 3:4], NEGM, ALU.is_lt, ALU.mult, [b_SC, b_thr], [b_negd])

                def cmp_branch(i, g):
                    pp = i % 2
                    ncts = 2 if i >= 8 else 1
                    ob = 2 + (o_n[0] % 2)
                    o_n[0] += 1
                    qflat = QnT[pp][g][:, :, :].rearrange("p a b -> p (a b)")
                    for ct in range(ncts):
                        sbk = ct % 2
                        mm(PB[sbk][:, :], KcT[:, ct * 128:(ct + 1) * 128], qflat, True, False, [b_KcT, b_QnT[pp]], [b_PB[sbk]])
                        mm(PB[sbk][:, :], cmpneg[pp][:, ct * 128:(ct + 1) * 128], i4[:], False, True, [b_const, b_cmpneg[pp]], [b_PB[sbk]])
                        e = et_n[0] % NET
                        et_n[0] += 1
                        act(ET[e][:], PB[sbk][:, :], AF.Exp, [b_PB[sbk]], [b_ET[e]], scale=0.125)
                        for r in range(4):
                            mm(PB[ob][:, r * 128:(r + 1) * 128], ET[e][:, r * 128:(r + 1) * 128], VcA[:, ct, g, :],
                               ct == 0 and r == 0, ct == ncts - 1 and r == 3, [b_ET[e], b_VcA], [b_PB[ob]])
                    act(Ocmp[g], PB[ob][:, :].rearrange("p (a b) -> p a b", b=128), AF.Copy, [b_PB[ob]], [b_Ocmp[g]])

                def selstats(i, g):
                    pp = i % 2
                    red(sm[:, 0:4], Ocmp[g][:, :, 64:128], ALU.add, [b_Ocmp[g]], [b_sm])
                    ts('dve', sm[:, 0:4], sm[:, 0:4], 1e-30, None, ALU.max, None, [b_sm], [b_sm])
                    recip(sm[:, 4 + 4 * g:8 + 4 * g], sm[:, 0:4], [b_sm], [b_sm])
                    tt('dve', impn[:], Ocmp[g][:, :, 64:128], sm[:, 4 + 4 * g:8 + 4 * g].unsqueeze(2).to_broadcast([128, 4, 64]),
                       ALU.mult, [b_Ocmp[g], b_sm], [b_impn])
                    red(imp[:], impn[:].rearrange("p r j -> p j r"), ALU.add, [b_impn], [b_imp])
                    tt('dve', imp[:], imp[:], tab[pp][:], ALU.add, [b_imp, b_tab[pp]], [b_imp])
                    S.op('dve', lambda e: e.max(out=m8[:, 0:8], in_=imp[:]), [b_imp], [b_m8])
                    S.op('dve', lambda e: e.match_replace(out=imp2, in_to_replace=m8[:, 0:8], in_values=imp[:],
                                                          imm_value=-3.0e38), [b_imp, b_m8], [b_imp, b_tmpB])
                    S.op('dve', lambda e: e.max(out=m8[:, 8:16], in_=imp2), [b_imp, b_tmpB], [b_m8])
                    ts('dve', nsel[g], imp[:], m8[:, 15:16], NEGM, ALU.is_lt, ALU.mult, [b_imp, b_m8], [b_nsel[g]])

                def expand(i, g):
                    nkeys = (2 * i + 2) * 128
                    nblk = nkeys // 64
                    act(negs[:, 0:nkeys].rearrange("p (a b) -> p a b", b=64),
                        nsel[g][:, 0:nblk].unsqueeze(2).to_broadcast([128, nblk, 64]), AF.Copy, [b_nsel[g]], [b_negs])

                def win_branch(i, g):
                    pp = i % 2
                    wkts = [kt for kt in range(2 * i - 4, 2 * i + 2) if kt >= 0]
                    attention(g, wkts, lambda kt: ([] if (kt - (2 * i - 4)) in (2, 3) else [(winneg[:, kt - (2 * i - 4), :], [])]),
                              1, lambda kt: Vaug[:, kt, 1, g, :], QnT[pp], b_QnT[pp], Owin[g], b_Owin[g])

                def sel_branch(i, g):
                    pp = i % 2

                    def selmask(kt):
                        ms = [(negs[:, kt * 128:(kt + 1) * 128], [b_negs])]
                        if kt == 2 * i:
                            ms.append((mab[:, 0:128], []))
                        if kt == 2 * i + 1:
                            ms.append((mab[:, 128:256], []))
                        return ms
                    attention(g, list(range(2 * i + 2)), selmask,
                              0, lambda kt: Vaug[:, kt, 0, g, :], QnT[pp], b_QnT[pp], Osel[g], b_Osel[g])

                def dsa_branch(i, g):
                    pp = i % 2
                    attention(g, list(range(2 * i + 2)),
                              lambda kt: [(negd[:, kt * 128:(kt + 1) * 128], [b_negd])],
                              2, lambda kt: Vaug[:, kt, 2, g, :], QdT[pp], b_QdT[pp], Odsa[g], b_Odsa[g])

                def combine_early(i):
                    pp = i % 2
                    Gv = G[pp][:].rearrange("p (h b) -> p h b", b=3)
                    cA, cB = kq[1], ktmp[0]
                    b_cA, b_cB = b_kq[1], b_ktmp[0]

                    def bc(a):
                        return coef[:, a:a + 8].unsqueeze(2).to_broadcast([128, 8, 64])
                    tt('dve', coef[:, 0:8], sm[:, 4:12], Gv[:, :, 0], ALU.mult, [b_sm, b_G[pp]], [b_coef])
                    recip(coef[:, 16:24], OwinA[:, :, 64], b_Owin, [b_coef])
                    tt('dve', coef[:, 16:24], coef[:, 16:24], Gv[:, :, 2], ALU.mult, [b_coef, b_G[pp]], [b_coef])
                    tt('dve', cA[:], OcmpA[:, :, 0:64], bc(0), ALU.mult, b_Ocmp + [b_coef], [b_cA])
                    tt('dve', cB[:], OwinA[:, :, 0:64], bc(16), ALU.mult, b_Owin + [b_coef], [b_cB])
                    tt('dve', cA[:], cA[:], cB[:], ALU.add, [b_cA, b_cB], [b_cA])

                def combine_dve(i):
                    pp = i % 2
                    Gv = G[pp][:].rearrange("p (h b) -> p h b", b=3)
                    cA, cB = kq[1], ktmp[0]
                    b_cA, b_cB = b_kq[1], b_ktmp[0]

                    def bc(a):
                        return coef[:, a:a + 8].unsqueeze(2).to_broadcast([128, 8, 64])
                    recip(coef[:, 8:16], OselA[:, :, 64], b_Osel, [b_coef])
                    tt('dve', coef[:, 8:16], coef[:, 8:16], Gv[:, :, 1], ALU.mult, [b_coef, b_G[pp]], [b_coef])
                    recip(coef[:, 24:32], OdsaA[:, :, 64], b_Odsa, [b_coef])
                    tt('dve', cB[:], OselA[:, :, 0:64], bc(8), ALU.mult, b_Osel + [b_coef], [b_cB])
                    tt('dve', cA[:], cA[:], cB[:], ALU.add, [b_cA, b_cB], [b_cA])
                    tt('dve', mixed[:, 0:512], cA[:].rearrange("p a b -> p (a b)"), Zs[pp][:, 0:512], ALU.mult,
                       [b_cA, b_Zs[pp]], [b_mixed])
                    tt('dve', cB[:], OdsaA[:, :, 0:64], bc(24), ALU.mult, b_Odsa + [b_coef], [b_cB])
                    tt('dve', mixed[:, 512:1024], cB[:].rearrange("p a b -> p (a b)"), Zs[pp][:, 512:1024], ALU.mult,
                       [b_cB, b_Zs[pp]], [b_mixed])

                def outproj(i):
                    pp = 1
                    dma(xt[1][:], x_own[i * 128:(i + 1) * 128, :], writes=[b_xt[1]])
                    for c in range(8):
                        tr(T0[:, c, :], mixed[:, c * 128:(c + 1) * 128], [b_mixed], [b_T0])
                    act(mixT[:], T0[:], AF.Copy, [b_T0], [b_mixT])
                    for nh in range(2):
                        bk = 4 + nh
                        for kc in range(8):
                            mm(PB[bk][:, :], mixT[:, kc, :], wOb[:, kc, nh * 512:(nh + 1) * 512], kc == 0, kc == 7,
                               [b_mixT, b_wO], [b_PB[bk]])
                        tt('dve', xt[pp][:, nh * 512:(nh + 1) * 512], PB[bk][:, :], xt[pp][:, nh * 512:(nh + 1) * 512], ALU.add,
                           [b_PB[bk], b_xt[pp]], [b_xt[pp]])
                    dma(y[i * 128:(i + 1) * 128, :], xt[pp][:], reads=[b_xt[pp]])

                prep_x_front(0)
                prep_x_back(0)
                prep_qi(0)
                prep_idx(0)
                search(0)
                search_final(0)
                prep_qproj(0)
                prep_qnorm_dve(0)
                prep_qnorm_pe(0)
                if nq > 1:
                    prep_x_front(1)
                cmp_branch(0, 0)
                cmp_branch(0, 1)
                win_branch(0, 0)
                win_branch(0, 1)
                for i in range(nq):
                    nxt = i + 1 < nq
                    if nxt:
                        prep_x_back(i + 1)
                        prep_qi(i + 1)
                        prep_qproj(i + 1)
                        prep_idx(i + 1)
                    selstats(i, 0)
                    expand(i, 0)
                    selstats(i, 1)
                    if nxt:
                        prep_qnorm_dve(i + 1)
                    combine_early(i)
                    if nxt:
                        search(i + 1)
                    dsa_branch(i, 0)
                    dsa_branch(i, 1)
                    if nxt:
                        search_final(i + 1)
                    sel_branch(i, 0)
                    expand(i, 1)
                    sel_branch(i, 1)
                    if nxt:
                        prep_qnorm_pe(i + 1)
                    combine_dve(i)
                    if i + 2 < nq:
                        prep_x_front(i + 2)
                    if nxt:
                        cmp_branch(i + 1, 0)
                        cmp_branch(i + 1, 1)
                        win_branch(i + 1, 0)
                        win_branch(i + 1, 1)
                    outproj(i)
        except _Stop:
            pass

        sems = {n: es.enter_context(nc.semaphore("s_" + n)) for n in ['pe', 'act', 'dve', 'pool']}
        dma_sems = [es.enter_context(nc.semaphore("d%d" % k)) for k in range(NDMA + NSW)]
        block = es.enter_context(nc.Block())
        S.emit(block, sems, dma_sems)
    return nc


def _consts(p):
    bf = ml_dtypes.bfloat16
    c = {}
    c["ident"] = np.eye(128, dtype=np.float32).astype(bf)
    c["i4"] = np.tile(np.eye(128, dtype=np.float32), (1, 4)).astype(bf)
    q = np.arange(128)[:, None]
    s = np.arange(128)[None, :]
    A = (s <= 128 * p + q)
    B = (128 + s <= 128 * p + q)
    mab = np.concatenate([np.where(A, 0.0, NEGM), np.where(B, 0.0, NEGM)], axis=1).astype(np.float32)
    c["mab"] = mab.astype(bf)
    c["mabf"] = np.where(mab < 0, -1e30, 0.0).astype(np.float32)
    win = np.zeros((128, 6, 128), np.float32)
    for j in range(6):
        diff = 128 * (p + 4 - j) + q - s
        win[:, j, :] = np.where((diff >= 0) & (diff < 512), 0.0, NEGM)
    c["winneg"] = win.astype(bf)
    rows = np.concatenate([np.arange(128 * (2 * i + p), 128 * (2 * i + p) + 128) for i in range(NQ)])
    t = rows[:, None]
    cc = np.arange(256)[None, :]
    c["cmpneg"] = np.where((16 * cc + 31 <= t) & (cc <= 254), 0.0, NEGM).astype(np.float32).astype(bf)
    c_start = np.arange(255) * 16
    j_start = np.arange(64) * 64
    ov = np.clip(np.minimum(c_start[:, None] + 32, j_start[None, :] + 64) - np.maximum(c_start[:, None], j_start[None, :]), 0, None)
    ovl = np.zeros((256, 64), np.float32)
    ovl[:255] = ov.astype(np.float32) / 32.0
    c["ovl"] = np.ascontiguousarray(ovl.reshape(2, 128, 64).transpose(1, 0, 2)).astype(bf)
    j = np.arange(64)[None, :]
    cur = t // 64
    forced = (j == 0) | (j == cur) | (j == cur - 1)
    c["nsatab"] = np.where(forced, 1e6, np.where(j * 64 <= t, 0.0, -1e30)).astype(np.float32)
    half = 32
    inv_freq = (np.float32(10000.0) ** (-np.arange(half, dtype=np.float32) / np.float32(half))).astype(np.float32)

    def cs_table(pos):
        ang = (pos.astype(np.float32)[:, None] * inv_freq[None, :]).astype(np.float32)
        co = np.cos(ang).astype(np.float32)
        si = np.sin(ang).astype(np.float32)
        return np.concatenate([co, co, -si, si], axis=1).astype(np.float32)
    cs_all = cs_table(np.arange(S_LEN))
    c["cs_all"] = cs_all
    c["cs_own"] = np.ascontiguousarray(cs_all[rows])
    cs_cmp = np.zeros((256, 128), np.float32)
    cs_cmp[:255] = cs_table(np.arange(255) * 16 + 31)
    c["cs_cmp"] = cs_cmp
    return c, rows


def _prep_weights(inp):
    f = np.float32
    w_in = np.asarray(inp["w_in"], f)[0]
    colsK = np.concatenate([np.arange(768, 896), np.arange(1024, 1152), np.arange(2328, 2456), np.arange(2840, 2904),
                            np.arange(512, 768), np.arange(896, 1024), np.arange(1152, 1280), np.arange(2456, 2584)])
    colsQ = np.concatenate([np.arange(0, 512), np.arange(1304, 1816), np.arange(1816, 2328), np.arange(2908, 3420),
                            np.arange(2584, 2840), np.arange(1280, 1304), np.arange(2904, 2908)])

    def kmaj(w):
        return np.ascontiguousarray(w.reshape(8, 128, w.shape[1]).transpose(1, 0, 2))
    d = {}
    d["wK"] = kmaj(w_in[:, colsK])
    d["wQ"] = kmaj(w_in[:, colsQ])
    d["wO"] = kmaj(np.asarray(inp["w_out"], f)[0])
    d["ng"] = np.ascontiguousarray(np.broadcast_to(np.asarray(inp["norm_gain"], f)[0][None, :], (128, 1024)))
    gks, gkw, gkd = (np.asarray(inp[n], f)[0] for n in ("nsa_ks_gain", "nsa_kw_gain", "dsa_k_gain"))
    d["gk"] = np.ascontiguousarray(np.broadcast_to(np.concatenate([gks, gks, gkw, gkw, gkd, gkd])[None, :], (128, 384)))
    gq = np.stack([np.asarray(inp[n], f)[0] for n in ("nsa_q_gain", "dsa_q_gain", "nsa_kc_gain")])
    d["gq"] = np.ascontiguousarray(np.broadcast_to(gq[None], (128, 3, 64)))
    w1 = []
    for n in ("cmp_k_w1", "cmp_v_w1"):
        w = np.asarray(inp[n], f)[0].reshape(32, 64, 256).transpose(1, 0, 2)
        w1.append(np.concatenate([w, w], axis=0))
    d["w1"] = np.ascontiguousarray(np.stack(w1))
    b1 = np.stack([np.asarray(inp[n], f)[0].reshape(2, 128) for n in ("cmp_k_b1", "cmp_v_b1")])
    d["b1"] = np.ascontiguousarray(b1.transpose(2, 0, 1).reshape(128, 4))
    w2 = np.stack([np.asarray(inp[n], f)[0].reshape(2, 128, 64) for n in ("cmp_k_w2", "cmp_v_w2")])
    d["w2"] = np.ascontiguousarray(w2.transpose(2, 0, 1, 3))
    pe = np.stack([np.asarray(inp[n], f)[0].T for n in ("cmp_pe_k", "cmp_pe_v")])
    pe = pe.transpose(1, 0, 2)
    d["peT"] = np.ascontiguousarray(np.concatenate([pe, pe], axis=0))
    return d


def make_in_maps(inp):
    x = np.asarray(inp["x"], np.float32)
    wd = _prep_weights(inp)
    maps = []
    for c in range(8):
        b, p = c // 2, c % 2
        cst, rows = _consts(p)
        m = dict(wd)
        m.update(cst)
        m["x_all"] = np.ascontiguousarray(x[b])
        m["x_own"] = np.ascontiguousarray(x[b][rows])
        maps.append((m, b, rows))
    return maps


_NC_CACHE = {}


def kernel(**inputs):
    if "nc" not in _NC_CACHE:
        _NC_CACHE["nc"] = build()
    nc = _NC_CACHE["nc"]
    maps = make_in_maps(inputs)
    res = run_bass_kernel_spmd(nc, [m for m, _, _ in maps], core_ids=list(range(8)))
    out = np.empty((4, S_LEN, 1024), np.float32)
    for c, (_, b, rows) in enumerate(maps):
        out[b][rows] = res.results[c]["y"]
    return out
```
